# Optimizing a Trainium2 kernel written in Bass

```python
import jax, jax.numpy as jnp
from jax import lax
import numpy as np

D_MODEL = 1024
BATCH = 32
SEQ = 2048
DEPTH = 1
DEC_BATCH = 32
DEC_SEQ = 16
PAST_LEN = 1024

CHUNK = 64
N_MEM = 256
NORM_EPS = 1e-6
Q_BLOCK = 128
GM_CHUNK = 128
GM_GROUPS = 8
GM_GROUP_DIM = D_MODEL // GM_GROUPS
D_GM = GM_GROUPS * GM_GROUP_DIM
MLA_HEADS = 8
Q_LORA = 384
KV_LORA = 256
NOPE_DIM = 128
ROPE_DIM = 64
V_DIM = 128
D_MLA = MLA_HEADS * V_DIM
ROPE_THETA = 10000.0
MLA_SCALE = (NOPE_DIM + ROPE_DIM) ** -0.5
MEM_HEADS = 4
MEM_HEAD_DIM = 256
D_MEM = MEM_HEADS * MEM_HEAD_DIM
MEM_SCALE = MEM_HEAD_DIM ** -0.5
N_BRANCH = 3
D_FF = -(-(8 * D_MODEL) // (3 * 256)) * 256
IN_SIZES = (D_GM, D_GM, Q_LORA, KV_LORA, ROPE_DIM, D_MEM, D_MODEL, D_MODEL, D_MODEL)
IN_SPLITS = tuple(np.cumsum(IN_SIZES)[:-1].tolist())
D_IN = sum(IN_SIZES)

kernel_name = 'hybrid_gmlp_mla_memory_stream_step'


def rmsnorm(x, g):
    xf = x.astype(jnp.float32)
    y = xf * lax.rsqrt(jnp.mean(xf * xf, axis=-1, keepdims=True) + NORM_EPS)
    return (y * g.astype(jnp.float32)).astype(x.dtype)


def apply_rope(x, pos):
    inv = ROPE_THETA ** (-jnp.arange(0, ROPE_DIM, 2, dtype=jnp.float32) / ROPE_DIM)
    ang = pos.astype(jnp.float32)[:, None] * inv[None, :]
    ang = ang.reshape((ang.shape[0],) + (1,) * (x.ndim - 3) + (ang.shape[1],))
    cos, sin = jnp.cos(ang), jnp.sin(ang)
    x1, x2 = jnp.split(x.astype(jnp.float32), 2, axis=-1)
    return jnp.concatenate([x1 * cos - x2 * sin, x1 * sin + x2 * cos], axis=-1).astype(x.dtype)


def chunk_causal_mask(q_pos, k_pos):
    return (k_pos[None, :] // CHUNK) <= (q_pos[:, None] // CHUNK)


def gmlp_spatial(u, v, w_s, b_s):
    B, T, _ = v.shape
    n_chunks = -(-T // GM_CHUNK)
    pad = n_chunks * GM_CHUNK - T
    vp = jnp.pad(v, ((0, 0), (0, pad), (0, 0))).reshape(B, n_chunks, GM_CHUNK, GM_GROUPS, GM_GROUP_DIM)
    causal = jnp.tril(jnp.ones((GM_CHUNK, GM_CHUNK), dtype=bool))
    w = jnp.where(causal[None], w_s, jnp.zeros_like(w_s))
    mixed = jnp.einsum('gpq,bnqgc->bnpgc', w, vp) + b_s.T[None, None, :, :, None]
    mixed = mixed.reshape(B, n_chunks * GM_CHUNK, D_GM)[:, :T]
    return u * mixed


def mla_block(q_nope, q_rope, q_pos, k_nope, k_rope, v, k_pos):
    s = (jnp.einsum('bqhd,bkhd->bhqk', q_nope, k_nope)
         + jnp.einsum('bqhr,bkr->bhqk', q_rope, k_rope)).astype(jnp.float32) * MLA_SCALE
    s = jnp.where(chunk_causal_mask(q_pos, k_pos)[None, None], s, -jnp.inf)
    p = jax.nn.softmax(s, axis=-1).astype(v.dtype)
    return jnp.einsum('bhqk,bkhd->bqhd', p, v)


def mla_attention(q_nope, q_rope, q_pos, k_nope, k_rope, v, k_pos):
    B, T, H, _ = q_nope.shape
    if T % Q_BLOCK:
        return mla_block(q_nope, q_rope, q_pos, k_nope, k_rope, v, k_pos)
    nb = T // Q_BLOCK

    def to_blocks(a):
        return a.reshape((B, nb, Q_BLOCK) + a.shape[2:]).swapaxes(0, 1)

    out = lax.map(lambda blk: mla_block(blk[0], blk[1], blk[2], k_nope, k_rope, v, k_pos),
                  (to_blocks(q_nope), to_blocks(q_rope), q_pos.reshape(nb, Q_BLOCK)))
    return out.swapaxes(0, 1).reshape(B, T, H, V_DIM)


def memory_kv(mem, g, w_kv):
    B = mem.shape[0]
    kv = jnp.einsum('bmd,de->bme', rmsnorm(mem, g), w_kv)
    k, v = jnp.split(kv, 2, axis=-1)
    return (k.reshape(B, N_MEM, MEM_HEADS, MEM_HEAD_DIM), v.reshape(B, N_MEM, MEM_HEADS, MEM_HEAD_DIM))


def memory_attend(qm, mem_k, mem_v):
    B, T, _ = qm.shape
    q = qm.reshape(B, T, MEM_HEADS, MEM_HEAD_DIM)
    s = jnp.einsum('bthd,bmhd->bhtm', q, mem_k).astype(jnp.float32) * MEM_SCALE
    p = jax.nn.softmax(s, axis=-1).astype(mem_v.dtype)
    return jnp.einsum('bhtm,bmhd->bthd', p, mem_v).reshape(B, T, D_MEM)


def mixing_sublayer(x, pos, lp, ckv_past, kr_past, mem_k, mem_v):
    B, T, _ = x.shape
    xn = rmsnorm(x, lp['norm_mix_g'])
    z = jnp.einsum('btd,de->bte', xn, lp['w_in'])
    u, v, cq, ckv, kr, qm, ga, gb, gc = jnp.split(z, IN_SPLITS, axis=-1)
    u = jax.nn.gelu(u)
    v = rmsnorm(jax.nn.gelu(v), lp['gm_norm_g'])
    o_gm = gmlp_spatial(u, v, lp['gm_ws'], lp['gm_bs'])
    q = jnp.einsum('btc,chd->bthd', rmsnorm(cq, lp['q_norm_g']), lp['w_uq'])
    q_nope = q[..., :NOPE_DIM]
    q_rope = apply_rope(q[..., NOPE_DIM:], pos)
    ckv = rmsnorm(ckv, lp['kv_norm_g'])
    kr = apply_rope(kr, pos)
    if ckv_past is None:
        ckv_all, kr_all, k_pos = ckv, kr, pos
    else:
        ckv_all = jnp.concatenate([ckv_past, ckv], axis=1)
        kr_all = jnp.concatenate([kr_past, kr], axis=1)
        k_pos = jnp.arange(ckv_all.shape[1], dtype=jnp.int32)
    k_nope = jnp.einsum('bkc,chd->bkhd', ckv_all, lp['w_uk'])
    v_mla = jnp.einsum('bkc,chd->bkhd', ckv_all, lp['w_uv'])
    o_mla = mla_attention(q_nope, q_rope, pos, k_nope, kr_all, v_mla, k_pos).reshape(B, T, D_MLA)
    o_mem = memory_attend(qm, mem_k, mem_v)
    merged = (jax.nn.sigmoid(ga) * (o_gm @ lp['w_br_gm'])
              + jax.nn.sigmoid(gb) * (o_mla @ lp['w_br_mla'])
              + jax.nn.sigmoid(gc) * (o_mem @ lp['w_br_mem']))
    y = merged @ lp['w_out']
    return x + y, ckv, kr, v


def ffn_sublayer(h, lp):
    hn = rmsnorm(h, lp['norm_ffn_g'])
    a = jax.nn.silu(hn @ lp['ffn_w_gate']) * (hn @ lp['ffn_w_up'])
    return h + a @ lp['ffn_w_down']


def setup_inputs(seed: int = 0) -> dict:
    key = jax.random.key(seed)
    ks = iter(jax.random.split(key, 32))
    nrm = lambda shape, scale: jax.random.normal(next(ks), shape, jnp.float32) * scale
    gain = lambda shape: 1.0 + 0.01 * jax.random.normal(next(ks), shape, jnp.float32)
    L = DEPTH
    return {
        'x_prompt': nrm((BATCH, SEQ, D_MODEL), 1.0),
        'x_sample': nrm((DEC_BATCH, DEC_SEQ, D_MODEL), 1.0),
        'cache_mla_ckv': nrm((L, DEC_BATCH, PAST_LEN, KV_LORA), 1.0),
        'cache_mla_krope': nrm((L, DEC_BATCH, PAST_LEN, ROPE_DIM), 1.0),
        'cache_mem_k': nrm((L, DEC_BATCH, N_MEM, MEM_HEADS, MEM_HEAD_DIM), 1.0),
        'cache_mem_v': nrm((L, DEC_BATCH, N_MEM, MEM_HEADS, MEM_HEAD_DIM), 1.0),
        'mem_prompt': nrm((BATCH, N_MEM, D_MODEL), 1.0),
        'norm_mix_g': gain((L, D_MODEL)),
        'w_in': nrm((L, D_MODEL, D_IN), D_MODEL ** -0.5),
        'gm_norm_g': gain((L, D_GM)),
        'gm_ws': nrm((L, GM_GROUPS, GM_CHUNK, GM_CHUNK), GM_CHUNK ** -0.5),
        'gm_bs': nrm((L, GM_GROUPS, GM_CHUNK), 0.1),
        'mla_q_norm_g': gain((L, Q_LORA)),
        'mla_w_uq': nrm((L, Q_LORA, MLA_HEADS, NOPE_DIM + ROPE_DIM), Q_LORA ** -0.5),
        'mla_kv_norm_g': gain((L, KV_LORA)),
        'mla_w_uk': nrm((L, KV_LORA, MLA_HEADS, NOPE_DIM), KV_LORA ** -0.5),
        'mla_w_uv': nrm((L, KV_LORA, MLA_HEADS, V_DIM), KV_LORA ** -0.5),
        'mem_norm_g': gain((L, D_MODEL)),
        'mem_w_kv': nrm((L, D_MODEL, 2 * D_MEM), D_MODEL ** -0.5),
        'w_br_gm': nrm((L, D_GM, D_MODEL), D_GM ** -0.5),
        'w_br_mla': nrm((L, D_MLA, D_MODEL), D_MLA ** -0.5),
        'w_br_mem': nrm((L, D_MEM, D_MODEL), D_MEM ** -0.5),
        'w_out': nrm((L, D_MODEL, D_MODEL), D_MODEL ** -0.5),
        'norm_ffn_g': gain((L, D_MODEL)),
        'ffn_w_gate': nrm((L, D_MODEL, D_FF), D_MODEL ** -0.5),
        'ffn_w_up': nrm((L, D_MODEL, D_FF), D_MODEL ** -0.5),
        'ffn_w_down': nrm((L, D_FF, D_MODEL), D_FF ** -0.5),
        'final_norm_g': gain((D_MODEL,)),
    }


def reference(x_prompt, x_sample, cache_mla_ckv, cache_mla_krope, cache_mem_k, cache_mem_v, mem_prompt,
              norm_mix_g, w_in, gm_norm_g, gm_ws, gm_bs, mla_q_norm_g, mla_w_uq, mla_kv_norm_g, mla_w_uk,
              mla_w_uv, mem_norm_g, mem_w_kv, w_br_gm, w_br_mla, w_br_mem, w_out, norm_ffn_g,
              ffn_w_gate, ffn_w_up, ffn_w_down, final_norm_g):
    pos_p = jnp.arange(x_prompt.shape[1], dtype=jnp.int32)
    pos_s = PAST_LEN + jnp.arange(x_sample.shape[1], dtype=jnp.int32)
    hp, hs = x_prompt, x_sample
    ckv_p_l, kr_p_l, mk_p_l, mv_p_l, ckv_s_l, kr_s_l, gv_s_l = [], [], [], [], [], [], []
    for l in range(DEPTH):
        lp = {'norm_mix_g': norm_mix_g[l], 'w_in': w_in[l], 'gm_norm_g': gm_norm_g[l],
              'gm_ws': gm_ws[l], 'gm_bs': gm_bs[l], 'q_norm_g': mla_q_norm_g[l], 'w_uq': mla_w_uq[l],
              'kv_norm_g': mla_kv_norm_g[l], 'w_uk': mla_w_uk[l], 'w_uv': mla_w_uv[l],
              'w_br_gm': w_br_gm[l], 'w_br_mla': w_br_mla[l], 'w_br_mem': w_br_mem[l], 'w_out': w_out[l],
              'norm_ffn_g': norm_ffn_g[l], 'ffn_w_gate': ffn_w_gate[l], 'ffn_w_up': ffn_w_up[l],
              'ffn_w_down': ffn_w_down[l]}
        mk_p, mv_p = memory_kv(mem_prompt, mem_norm_g[l], mem_w_kv[l])
        hp, ckv_p, kr_p, _ = mixing_sublayer(hp, pos_p, lp, None, None, mk_p, mv_p)
        hp = ffn_sublayer(hp, lp)
        hs, ckv_s, kr_s, gv_s = mixing_sublayer(hs, pos_s, lp, cache_mla_ckv[l], cache_mla_krope[l],
                                                cache_mem_k[l], cache_mem_v[l])
        hs = ffn_sublayer(hs, lp)
        ckv_p_l.append(ckv_p); kr_p_l.append(kr_p); mk_p_l.append(mk_p); mv_p_l.append(mv_p)
        ckv_s_l.append(ckv_s); kr_s_l.append(kr_s); gv_s_l.append(gv_s)
    y_prompt = rmsnorm(hp, final_norm_g)
    y_sample = rmsnorm(hs, final_norm_g)
    new_mla_ckv_prompt = jnp.stack(ckv_p_l)
    new_mla_krope_prompt = jnp.stack(kr_p_l)
    new_mem_k_prompt = jnp.stack(mk_p_l)
    new_mem_v_prompt = jnp.stack(mv_p_l)
    new_mla_ckv_sample = jnp.stack(ckv_s_l)
    new_mla_krope_sample = jnp.stack(kr_s_l)
    new_gm_v_sample = jnp.stack(gv_s_l)
    return (y_prompt, y_sample, new_mla_ckv_prompt, new_mla_krope_prompt, new_mem_k_prompt,
            new_mem_v_prompt, new_mla_ckv_sample, new_mla_krope_sample, new_gm_v_sample)
```

```python
import numpy as np
import concourse.bass as bass
import concourse.mybir as mybir
from concourse.bass_utils import run_bass_kernel_spmd

F32 = mybir.dt.float32
BF16 = mybir.dt.bfloat16
AF = mybir.ActivationFunctionType
ALU = mybir.AluOpType
AX = mybir.AxisListType


class Tk:
    __slots__ = ("name", "w", "r", "sem", "semcnt")

    def __init__(self, name):
        self.name = name
        self.w = None
        self.r = []
        self.sem = None
        self.semcnt = 0


class Op:
    __slots__ = ("eng", "emit", "deps", "signal", "sigval", "sem", "is_dma", "nparts", "sig_tile", "idx", "label")

    def __init__(self, eng, emit, is_dma=False, nparts=1, sig_tile=None):
        self.eng = eng
        self.emit = emit
        self.deps = []
        self.signal = False
        self.sigval = None
        self.sem = None
        self.is_dma = is_dma
        self.nparts = nparts
        self.sig_tile = sig_tile


ENGS = ("pe", "act", "dve", "pool", "sp")


class Prog:
    def __init__(self, nc, same_engine_sync=True):
        self.nc = nc
        self.ops = {e: [] for e in ENGS}
        self.same_engine_sync = same_engine_sync
        self.final_ops = []
        self.n_tk = 0

    def tk(self, name=None):
        self.n_tk += 1
        return Tk(name or f"t{self.n_tk}")

    def _add(self, op, reads, writes):
        deps = []
        seen = set()
        for t in reads:
            if t.w is not None and id(t.w) not in seen:
                seen.add(id(t.w)); deps.append(t.w)
        for t in writes:
            if t.w is not None and id(t.w) not in seen:
                seen.add(id(t.w)); deps.append(t.w)
            for r in t.r:
                if id(r) not in seen:
                    seen.add(id(r)); deps.append(r)
        for d in deps:
            if d is op:
                continue
            if (not d.is_dma) and (not op.is_dma) and d.eng == op.eng:
                if d.eng == "pe":
                    continue
                if not self.same_engine_sync:
                    continue
            d.signal = True
            op.deps.append(d)
        for t in reads:
            if not op.is_dma:
                t.r = [r for r in t.r if r.is_dma or r.eng != op.eng]
            t.r.append(op)
        for t in writes:
            t.w = op
            t.r = []
        self.ops[op.eng].append(op)
        self.n_ops = getattr(self, "n_ops", 0) + 1
        op.idx = self.n_ops
        op.label = getattr(self, "label", "")
        return op

    def op(self, eng, emit, reads=(), writes=()):
        return self._add(Op(eng, emit), list(reads), list(writes))

    def dma(self, eng, pairs, reads=(), writes=(), sig=None):
        assert sig is not None

        def emit(e, pairs=pairs):
            return [e.dma_start(out=o, in_=i) for (o, i) in pairs]
        op = Op(eng, emit, is_dma=True, nparts=len(pairs), sig_tile=sig)
        op.signal = True
        return self._add(op, list(reads), list(writes))

    def finalize(self, final_wait_ops):
        nc = self.nc
        import contextlib
        with contextlib.ExitStack() as st:
            final_wait_ops = list(final_wait_ops)
            for e in ENGS:
                if self.ops[e]:
                    last = self.ops[e][-1]
                    last.signal = True
                    final_wait_ops.append(last)
                final_wait_ops += [o for o in self.ops[e] if o.is_dma]
            esem = {e: st.enter_context(nc.semaphore(f"S_{e}")) for e in ENGS if e != "sp"}
            cnt = {e: 0 for e in ENGS}
            tiles_with_sem = {}
            all_dma = sorted([o for e in ENGS for o in self.ops[e] if o.is_dma], key=lambda o: o.idx)
            for op in all_dma:
                t = op.sig_tile
                if t.sem is None:
                    t.sem = {}
                    t.semcnt = {}
                if op.eng not in t.sem:
                    t.sem[op.eng] = st.enter_context(nc.semaphore(f"D_{t.name}_{op.eng}"))
                    t.semcnt[op.eng] = 0
                    tiles_with_sem[(id(t), op.eng)] = t
                t.semcnt[op.eng] += 16 * op.nparts
                op.sem = t.sem[op.eng]
                op.sigval = t.semcnt[op.eng]
            for e in ENGS:
                for op in self.ops[e]:
                    if op.is_dma:
                        pass
                    elif op.signal:
                        cnt[e] += 1
                        op.sem = esem[e]
                        op.sigval = cnt[e]
            self.n_sems = len(tiles_with_sem) + 4
            block = st.enter_context(nc.Block())
            handles = {"pe": block.tensor, "act": block.scalar, "dve": block.vector,
                       "pool": block.gpsimd, "sp": block.sync}
            for e in ENGS:
                ops = self.ops[e]
                fin = final_wait_ops if e == "sp" else []

                def body(eng, ops=ops, fin=fin):
                    waited = {}

                    def do_waits(deps):
                        need = {}
                        for d in deps:
                            k = id(d.sem)
                            if k not in need or need[k][1] < d.sigval:
                                need[k] = (d.sem, d.sigval)
                        for k, (sem, v) in need.items():
                            if waited.get(k, 0) >= v:
                                continue
                            waited[k] = v
                            eng.wait_ge(sem, v)
                    for op in ops:
                        do_waits(op.deps)
                        r = op.emit(eng)
                        if op.is_dma:
                            for ins in r:
                                ins.then_inc(op.sem, 16)
                        elif op.signal:
                            r.then_inc(op.sem, 1)
                    do_waits(fin)
                handles[e](body)


D = 1024
DFF = 2816
QL, KVL, ROPE, NOPE, VD = 384, 256, 64, 128, 128
H = 8
NMEM = 256
EPS = 1e-6
MLA_SCALE = float((NOPE + ROPE) ** -0.5)
MEM_SCALE = float(256 ** -0.5)
DEC = 16
N_CORES = 8
NSLOT = 3

C_U, C_V, C_CQ, C_QM, C_GA, C_GB, C_GC = 0, 1024, 2048, 2752, 3776, 4800, 5824


def build_program(NP, SEQ, NS, PAST, dbg=False):
    import contextlib
    import os as _os
    nc = bass.Bass("TRN2", target_bir_lowering=False)
    NT = SEQ // 512
    NKT_S = PAST // 128
    TS_S = NS * DEC
    assert TS_S <= 128 and PAST % 512 == 0

    def din(name, shape, dt=F32):
        return nc.dram_tensor(name, list(shape), dt, kind="ExternalInput").ap()

    def dout(name, shape):
        return nc.dram_tensor(name, list(shape), F32, kind="ExternalOutput").ap()

    x_p = din("x_p", [NP * SEQ, D]); x_s = din("x_s", [TS_S, D])
    c_ckv = din("c_ckv", [NS, PAST, KVL]); c_kr = din("c_kr", [NS, PAST, ROPE])
    c_mk = din("c_mk", [NS, NMEM, D]); c_mv = din("c_mv", [NS, NMEM, D])
    mem_p = din("mem_p", [NP, NMEM, D])
    g_mix = din("g_mix", [D]); g_gm = din("g_gm", [D]); g_q = din("g_q", [QL]); g_kv = din("g_kv", [KVL])
    g_mem = din("g_mem", [D]); g_ffn = din("g_ffn", [D]); g_fin = din("g_fin", [D])
    w_in = din("w_in", [D, 6848]); gm_ws = din("gm_ws", [8, 128, 128]); gm_bs = din("gm_bs", [8, 128])
    w_uq = din("w_uq", [QL, H * 192]); w_uk = din("w_uk", [KVL, D]); w_uv = din("w_uv", [KVL, D])
    w_mkv = din("w_mkv", [D, 2 * D]); w_bgm = din("w_bgm", [D, D]); w_bmla = din("w_bmla", [D, D])
    w_bmem = din("w_bmem", [D, D]); w_o = din("w_o", [D, D])
    w_g = din("w_g", [D, DFF]); w_u = din("w_u", [D, DFF]); w_d = din("w_d", [DFF, D])
    ident_d = din("ident", [128, 128]); maskT_d = din("maskT", [128, 128])
    rq_p = din("rq_p", [2, 128, SEQ]); rk_p = din("rk_p", [2, SEQ, 64])
    rq_s = din("rq_s", [2, 128, TS_S]); rk_s = din("rk_s", [2, TS_S, 64])
    y_p = dout("y_p", [NP * SEQ, D]); y_s = dout("y_s", [TS_S, D])
    ckv_p = dout("ckv_p", [NP * SEQ, KVL]); kr_p = dout("kr_p", [NP * SEQ, ROPE])
    mk_p = dout("mk_p", [NP * NMEM, D]); mv_p = dout("mv_p", [NP * NMEM, D])
    ckv_s = dout("ckv_s", [TS_S, KVL]); kr_s = dout("kr_s", [TS_S, ROPE]); gv_s = dout("gv_s", [TS_S, D])

    st = contextlib.ExitStack()
    with st:
        P = Prog(nc)

        def sb(name, shape, dt):
            return st.enter_context(nc.sbuf_tensor("sb_" + name, list(shape), dt))

        chunks = {}
        prep_pairs = []

        def kview(src):
            return src.rearrange("(kc p) n -> p kc n", p=128)

        def mkchunk(name, KC, NC, pieces):
            t = nc.dram_tensor("ws_" + name, [128, KC * NC], BF16).ap()
            tv = t.rearrange("p (kc n) -> p kc n", kc=KC)
            for (kc0, kcn, c0, src) in pieces:
                n = src.shape[1]
                prep_pairs.append((name, tv[:, kc0:kc0 + kcn, c0:c0 + n], kview(src)))
            chunks[name] = (t, KC, NC)

        for j in range(2):
            mkchunk(f"v{j}", 8, 512, [(0, 8, 0, w_in[:, C_V + 512 * j:C_V + 512 * (j + 1)])])
            mkchunk(f"u{j}", 8, 512, [(0, 8, 0, w_in[:, C_U + 512 * j:C_U + 512 * (j + 1)])])
            mkchunk(f"c{j}", 8, 352, [(0, 8, 0, w_in[:, C_CQ + 352 * j:C_CQ + 352 * (j + 1)])])
            mkchunk(f"qm{j}", 8, 512, [(0, 8, 0, w_in[:, C_QM + 512 * j:C_QM + 512 * (j + 1)])])
            mkchunk(f"o{j}", 8, 512, [(0, 8, 0, w_o[:, 512 * j:512 * (j + 1)])])
        for j in range(4):
            for nm, cg, wb in (("ga", C_GA, w_bgm), ("gb", C_GB, w_bmla), ("gc", C_GC, w_bmem)):
                mkchunk(f"{nm}{j}", 8, 512, [(0, 8, 0, w_in[:, cg + 256 * j:cg + 256 * (j + 1)]),
                                             (0, 8, 256, wb[:, 256 * j:256 * (j + 1)])])
        mkchunk("q0", 3, 1024, [(0, 3, h * 128, w_uq[:, h * 192:h * 192 + 128]) for h in range(H)])
        pcs = []
        for h in range(H):
            b = h * 192 + 128
            pcs.append((0, 3, h * 64, w_uq[:, b:b + 64]))
            pcs.append((0, 3, 512 + h * 64, w_uq[:, b + 32:b + 64]))
            pcs.append((0, 3, 512 + h * 64 + 32, w_uq[:, b:b + 32]))
        mkchunk("q1", 3, 1024, pcs)
        mkchunk("kv", 4, 1024, [(0, 2, 0, w_uk), (2, 2, 0, w_uv)])
        for j in range(11):
            mkchunk(f"f{j}", 8, 512, [(0, 8, 0, w_g[:, 256 * j:256 * (j + 1)]), (0, 8, 256, w_u[:, 256 * j:256 * (j + 1)])])
        KG = [8, 8, 6]
        for nh in range(2):
            for kg in range(3):
                mkchunk(f"fd{nh}{kg}", KG[kg], 512,
                        [(0, KG[kg], 0, w_d[kg * 1024:kg * 1024 + KG[kg] * 128, 512 * nh:512 * (nh + 1)])])
        for j in range(4):
            mkchunk(f"m{j}", 8, 512, [(0, 8, 0, w_mkv[:, 512 * j:512 * (j + 1)])])

        TILE_ORDER = (["v0", "v1", "u0", "u1", "ga0", "ga1", "ga2", "ga3",
                       "c0", "c1", "q0", "q1", "kv", "gb0", "gb1", "gb2", "gb3",
                       "qm0", "qm1", "gc0", "gc1", "gc2", "gc3", "o0", "o1"]
                      + [f"f{j}" for j in range(11)]
                      + [f"fd{nh}{kg}" for nh in range(2) for kg in range(3)])
        wseq = []
        for b in range(NP):
            wseq += ["m0", "m1", "m2", "m3"] + TILE_ORDER * NT
        wseq += TILE_ORDER

        grp_lists = [["m0", "m1", "m2", "m3"],
                     ["v0", "v1", "u0", "u1", "ga0", "ga1", "ga2", "ga3"],
                     ["c0", "c1", "q0", "q1", "kv", "gb0", "gb1", "gb2", "gb3"],
                     ["qm0", "qm1", "gc0", "gc1", "gc2", "gc3", "o0", "o1"],
                     [f"f{j}" for j in range(11)],
                     [f"fd{nh}{kg}" for nh in range(2) for kg in range(3)]]
        chunk_grp = {nm: g for g, l in enumerate(grp_lists) for nm in l}
        assert set(chunk_grp) == set(chunks)
        T_prep_g = [P.tk(f"prep{g}") for g in range(len(grp_lists))]
        prep_issued = set()

        def issue_prep(g):
            if g in prep_issued:
                return
            prep_issued.add(g)
            prs = [(o, i) for (nm, o, i) in prep_pairs if chunk_grp[nm] == g]
            P.dma("pool", prs, writes=[T_prep_g[g]], sig=T_prep_g[g])
        issue_prep(0)
        issue_prep(1)
        wslot = [sb(f"wslot{i}", [128, 4096], BF16) for i in range(NSLOT)]
        wslot_tk = [P.tk(f"wslot{i}") for i in range(NSLOT)]
        wstate = {"next_load": 0, "next_get": 0}

        def w_load_next():
            i = wstate["next_load"]
            if i >= len(wseq):
                return
            wstate["next_load"] += 1
            t, KC, NC = chunks[wseq[i]]
            s = i % NSLOT
            assert chunk_grp[wseq[i]] in prep_issued, wseq[i]
            P.dma("sp", [(wslot[s][:, 0:KC * NC], t)], reads=[T_prep_g[chunk_grp[wseq[i]]]], writes=[wslot_tk[s]], sig=wslot_tk[s])

        class WC:
            pass

        def w_get(expect):
            i = wstate["next_get"]
            assert wseq[i] == expect, (wseq[i], expect)
            wstate["next_get"] += 1
            while wstate["next_load"] <= i:
                w_load_next()
            t, KC, NC = chunks[wseq[i]]
            s = i % NSLOT
            c = WC()
            c.ap = wslot[s][:, 0:KC * NC].rearrange("p (kc n) -> p kc n", kc=KC)
            c.tk = wslot_tk[s]
            c.KC, c.NC = KC, NC
            return c

        def w_load_next_if_room():
            while wstate["next_load"] < len(wseq) and wstate["next_load"] < wstate["released"] + NSLOT:
                w_load_next()
        wstate["released"] = 0

        def w_rel(c):
            wstate["released"] += 1
            w_load_next_if_room()

        psb = [st.enter_context(nc.psum_tensor(f"ps{i}", [128, 512], F32)) for i in range(8)]
        ps_tk = [P.tk(f"ps{i}") for i in range(8)]
        psstate = {"i": 0, "held": set()}
        ps_idx = {id(t): i for i, t in enumerate(ps_tk)}

        def ps_next():
            for _ in range(9):
                i = psstate["i"]
                psstate["i"] = (i + 1) % 8
                if i not in psstate["held"]:
                    return psb[i], ps_tk[i]
            raise RuntimeError("all PSUM banks held")

        def ps_hold():
            pb, pt = ps_next()
            psstate["held"].add(ps_idx[id(pt)])
            return pb, pt

        def ps_rel(*banks):
            for (pb, pt) in banks:
                psstate["held"].discard(ps_idx[id(pt)])

        ident_f = sb("ident_f", [128, 128], F32); ident = sb("ident", [128, 128], BF16)
        maskT = sb("maskT", [128, 128], F32)
        ones = sb("ones", [128, 128], BF16)
        mhalf = sb("mhalf", [128, 8], F32)
        g_gm_bc = sb("g_gm_bc", [128, D], F32); g_fin_bc = sb("g_fin_bc", [128, D], F32)
        g_kv_bc = sb("g_kv_bc", [128, KVL], F32)
        gcol = sb("gcol", [128, 32], F32)
        WsT = sb("WsT", [128, 8, 128], BF16); WsT_s = sb("WsT_s", [64, 8, 64], BF16)
        b_row = sb("b_row", [128, 8, 128], BF16); b_row_s = sb("b_row_s", [128, 8, 64], BF16)
        Kn = sb("Kn", [128, H, 2048], BF16); V = sb("V", [128, 16, D], BF16); KrT = sb("KrT", [128, 2048], BF16)
        KmT = sb("KmT", [128, 8, NMEM], BF16); Vm = sb("Vm", [128, 2, D], BF16)
        HB = sb("HB", [128, 4, D], F32)
        xnT = sb("xnT", [128, 8, 512], BF16)
        mrg_f = sb("mrg", [128, 2, D], F32)
        mrg = mrg_f[:].rearrange("p s d -> p (s d)").bitcast(BF16).rearrange("p (a b) -> p a b", a=8)
        AR = sb("AR", [128, 3, 8, 512], BF16)
        xn_st = [sb(f"xn_st{i}", [128, D], BF16) for i in range(2)]
        ckv_o = [sb(f"ckv_o{i}", [128, KVL], F32) for i in range(2)]
        kr_o = [sb(f"kr_o{i}", [128, ROPE], F32) for i in range(2)]
        cq_n2 = [sb(f"cq_n{i}", [128, QL], BF16) for i in range(2)]
        ckv_nb2 = [sb(f"ckv_nb{i}", [128, KVL], BF16) for i in range(2)]
        kr_b2 = [sb(f"kr_b{i}", [128, 128], BF16) for i in range(2)]
        rope_t = sb("rope_t", [128, 2, 64], F32)
        cq_nT = sb("cq_nT", [128, 3, 512], BF16); ckv_nT = sb("ckv_nT", [128, 2, 512], BF16)
        krT_t = sb("krT_t", [128, 512], BF16)
        rq = sb("rq", [128, 2, 512], F32); rk = sb("rk", [128, 2, 4, 64], F32)
        PT = [sb(f"PT{i}", [128, 512], BF16) for i in range(4)]
        SC = [sb(f"SC{i}", [128, 512], F32) for i in range(4)]
        stats = sb("stats", [128, 64], F32)
        cst = HB[:, 3, :]

        tk = P.tk
        T_ident, T_maskT, T_ones, T_mh = tk("ident"), tk("maskT"), tk("ones"), tk("mh")
        T_gbc, T_gcol, T_WsT, T_brow = tk("gbc"), tk("gcol"), tk("WsT"), tk("brow")
        T_Kn, T_V, T_KrT, T_KmT, T_Vm = tk("Kn"), tk("V"), tk("KrT"), tk("KmT"), tk("Vm")
        T_hb = [tk(f"hb{i}") for i in range(4)]
        T_xnT, T_mrg, T_mrgB = tk("xnT"), tk("mrg"), tk("mrgB")
        T_A = [tk(f"A{i}") for i in range(3)]
        T_xn = [tk("xn0"), tk("xn1")]
        T_ckvo = [tk("ckvo0"), tk("ckvo1")]; T_kro = [tk("kro0"), tk("kro1")]
        T_cqn2, T_ckvnb2, T_krb2 = [tk("cqn0"), tk("cqn1")], [tk("ckvnb0"), tk("ckvnb1")], [tk("krb0"), tk("krb1")]
        T_ropet = tk("ropet")
        T_cqnT, T_ckvnT, T_krTt = tk("cqnT"), tk("ckvnT"), tk("krTt")
        T_rq, T_rk = tk("rq"), tk("rk")
        T_PT = [tk(f"PT{i}") for i in range(4)]
        T_SC = [tk(f"SC{i}") for i in range(4)]
        T_cst = T_hb[3]
        stat_tk = {}

        def stat(name, n):
            if name not in stat_tk:
                off = stat_tk.get("_off", 0)
                stat_tk["_off"] = off + n
                assert off + n <= 64
                stat_tk[name] = (stats[:, off:off + n], tk("st_" + name))
            return stat_tk[name]

        def MM(out, lhsT, rhs, start, stop, rd, wr):
            P.op("pe", lambda e: e.matmul(out, lhsT=lhsT, rhs=rhs, start=start, stop=stop), reads=rd, writes=wr)

        def TR(out, in_, n, rd, wr):
            P.op("pe", lambda e: e.transpose(out=out, in_=in_, identity=ident[0:n, 0:n]), reads=rd + [T_ident], writes=wr)

        def ACT(out, in_, func, rd, wr, scale=1.0, accum=None, bias=None):
            kw = {}
            if accum is not None:
                kw["accum_out"] = accum
            if bias is not None:
                kw["bias"] = bias
            P.op("act", lambda e: e.activation(out=out, in_=in_, func=func, scale=scale, **kw), reads=rd, writes=wr)

        def CP(eng, out, in_, rd, wr):
            if eng == "act":
                P.op("act", lambda e: e.copy(out=out, in_=in_), reads=rd, writes=wr)
            else:
                P.op(eng, lambda e: e.tensor_copy(out=out, in_=in_), reads=rd, writes=wr)

        def TSC(eng, out, in0, s1, s2, op0, op1, rd, wr):
            if s2 is None:
                P.op(eng, lambda e: e.tensor_scalar(out=out, in0=in0, scalar1=s1, scalar2=None, op0=op0), reads=rd, writes=wr)
            else:
                P.op(eng, lambda e: e.tensor_scalar(out=out, in0=in0, scalar1=s1, scalar2=s2, op0=op0, op1=op1), reads=rd, writes=wr)

        def TT(eng, out, in0, in1, op, rd, wr):
            P.op(eng, lambda e: e.tensor_tensor(out=out, in0=in0, in1=in1, op=op), reads=rd, writes=wr)

        def STT(eng, out, in0, scalar, in1, op0, op1, rd, wr):
            P.op(eng, lambda e: e.scalar_tensor_tensor(out=out, in0=in0, scalar=scalar, in1=in1, op0=op0, op1=op1),
                 reads=rd, writes=wr)

        cp_rr = {"i": 0}

        def CPrr(out, in_, rd, wr):
            cp_rr["i"] ^= 1
            CP("act" if cp_rr["i"] else "dve", out, in_, rd, wr)

        def rstd(ss_ap, n, t_ss):
            TSC("dve", ss_ap, ss_ap, EPS, None, ALU.add, None, [t_ss], [t_ss])
            TT("pool", ss_ap, ss_ap, mhalf[0:ss_ap.shape[0], 0:n], ALU.pow, [t_ss, T_mh], [t_ss])

        P.dma("sp", [(ident_f[:], ident_d), (maskT[:], maskT_d)], writes=[T_ident, T_maskT], sig=T_ident)
        CP("dve", ident[:], ident_f[:], [T_ident], [T_ident])
        P.op("dve", lambda e: e.memset(ones[:], 1.0), writes=[T_ones])
        P.op("dve", lambda e: e.memset(mhalf[:], -0.5), writes=[T_mh])
        P.dma("sp", [(g_gm_bc[:], g_gm.partition_broadcast(128)), (g_fin_bc[:], g_fin.partition_broadcast(128)),
                     (g_kv_bc[:], g_kv.partition_broadcast(128))], writes=[T_gbc], sig=T_gbc)

        def rowview(g, n):
            return g.rearrange("(kc p) -> kc p", p=128)
        P.dma("sp", [(cst[0:8, 0:128], rowview(g_mix, 8)), (cst[8:16, 0:128], rowview(g_ffn, 8)),
                     (cst[16:24, 0:128], rowview(g_mem, 8)), (cst[24:27, 0:128], rowview(g_q, 3))], writes=[T_cst], sig=T_cst)
        pb, pt = ps_next()
        P.op("pe", lambda e, pb=pb: e.transpose(out=pb[:, 0:27], in_=cst[0:27, 0:128], identity=ident_f[0:27, 0:27]),
             reads=[T_cst, T_ident], writes=[pt])
        CP("dve", gcol[:, 0:27], pb[:, 0:27], [pt], [T_gcol])
        for g in range(8):
            P.dma("sp", [(cst[:, 0:128], gm_ws[g])], writes=[T_cst], sig=T_cst)
            pb, pt = ps_next()
            P.op("pe", lambda e, pb=pb: e.transpose(out=pb[:, 0:128], in_=cst[:, 0:128], identity=ident_f[:]),
                 reads=[T_cst, T_ident], writes=[pt])
            TT("dve", WsT[:, g, :], pb[:, 0:128], maskT[:], ALU.mult, [pt, T_maskT], [T_WsT])
        P.op("dve", lambda e: e.memset(cst[0:64, 0:512], 0.0), reads=[], writes=[T_cst])
        bp = []
        for g in range(8):
            for s in range(NS):
                bp.append((cst[16 * s:16 * s + 16, g * 64 + 16 * s:g * 64 + 16 * s + 16], gm_ws[g, 0:16, 0:16]))
        P.dma("sp", bp, writes=[T_cst], sig=T_cst)
        for g in range(8):
            pb, pt = ps_next()
            P.op("pe", lambda e, pb=pb, g=g: e.transpose(out=pb[0:64, 0:64], in_=cst[0:64, g * 64:(g + 1) * 64],
                                                         identity=ident_f[0:64, 0:64]),
                 reads=[T_cst, T_ident], writes=[pt])
            TT("dve", WsT_s[:, g, :], pb[0:64, 0:64], maskT[0:64, 0:64], ALU.mult, [pt, T_maskT], [T_WsT])
        P.dma("sp", [(cst[0:1, 0:1024], gm_bs.rearrange("g p -> (g p)").rearrange("(o n) -> o n", o=1))],
              writes=[T_cst], sig=T_cst)
        P.op("dve", lambda e: e.memset(b_row[:], 0.0), reads=[], writes=[T_brow])
        CP("dve", b_row[0:1].rearrange("o g p -> o (g p)"), cst[0:1, 0:1024], [T_cst], [T_brow])
        bp = []
        for s in range(NS):
            bp.append((cst[0:1, 0:512].rearrange("o (g p) -> o g p", g=8)[:, :, 16 * s:16 * s + 16],
                       gm_bs[:, 0:16].rearrange("(o g) p -> o g p", o=1)))
        P.dma("sp", bp, writes=[T_cst], sig=T_cst)
        P.op("dve", lambda e: e.memset(b_row_s[:], 0.0), reads=[], writes=[T_brow])
        CP("dve", b_row_s[0:1].rearrange("o g p -> o (g p)"), cst[0:1, 0:512], [T_cst], [T_brow])

        out_ops = []
        import os as _os
        STOP = int(_os.environ.get("KSTOP", "999"))
        stg = {"n": 0}

        class _Stop(Exception):
            pass

        def stage(name):
            P.label = name
            stg["n"] += 1
            if stg["n"] > STOP:
                print("STOP at stage", stg["n"], name)
                raise _Stop()

        def norm_transpose(nst, TS, src_aps, src_tks, col0, ss_name, dstT, T_dst, ncols_off=0):
            ss, t_ss = stat(ss_name, 4)
            banks = [ps_hold() for _ in range(4)]
            for s_ in range(nst):
                src, tsrc = src_aps[s_], src_tks[s_]
                xb, txb = xn_st[s_ % 2], T_xn[s_ % 2]
                ACT(xb[0:TS, :], src, AF.Square, [tsrc], [txb, t_ss], scale=1.0 / 32, accum=ss[0:TS, s_:s_ + 1])
                rstd(ss[0:TS, s_:s_ + 1], 1, t_ss)
                TSC("dve", xb[0:TS, :], src, ss[0:TS, s_:s_ + 1], None, ALU.mult, None, [tsrc, t_ss], [txb])
                for kc in range(8):
                    pb, pt = banks[kc // 2]
                    pbb = pb[:].bitcast(BF16)
                    o = (kc % 2) * 512 + s_ * TS
                    TR(pbb[:, o:o + TS], xb[0:TS, kc * 128:(kc + 1) * 128], TS, [txb], [pt])
            TTn = nst * TS
            ps_rel(*banks)
            stage("nt_transposes")
            for kc in range(8):
                pb, pt = banks[kc // 2]
                pbb = pb[:].bitcast(BF16)
                o = (kc % 2) * 512
                dst = dstT[:, kc, ncols_off:ncols_off + TTn]
                if kc % 2 == 0:
                    TSC("dve", dst, pbb[:, o:o + TTn], gcol[:, col0 + kc:col0 + kc + 1], None, ALU.mult, None,
                        [pt, T_gcol], [T_dst])
                else:
                    ACT(dst, pbb[:, o:o + TTn], AF.Copy, [pt, T_gcol], [T_dst], scale=gcol[:, col0 + kc:col0 + kc + 1])

        def fm_proj(wc, ocs, rhsT, T_rhs, KC, TTn, evac):
            for (ocl, oc) in ocs:
                pb, pt = ps_next()
                for kc in range(KC):
                    MM(pb[:, 0:TTn], wc.ap[:, kc, ocl * 128:(ocl + 1) * 128], rhsT[:, kc, 0:TTn], kc == 0, kc == KC - 1,
                       [wc.tk, T_rhs], [pt])
                evac(oc, pb, pt)

        def attention(h_list, qn_of, qr_of, nq, keytiles, out_of, scale, kind):
            nk_t = len(keytiles)
            items = [(hi, h, idx) for hi, h in enumerate(h_list) for idx in range(nk_t)]

            def issue_S(g):
                hi, h, idx = items[g]
                j, nk, c0, diag = keytiles[idx]
                b = g % 4
                sbk, stk = psb[b], ps_tk[b]
                qn, tqn = qn_of(h)
                qr, tqr, hb_ = qr_of(h)
                MM(sbk[0:nk, c0:nq], Kn[:, h, j * 128:j * 128 + nk], qn[:, c0:nq], True, False, [T_Kn, tqn], [stk])
                MM(sbk[0:nk, c0:nq], KrT[:, j * 128:j * 128 + nk], qr[:, c0:nq], False, True, [T_KrT, tqr], [stk])
                ACT(PT[b][0:nk, c0:nq], sbk[0:nk, c0:nq], AF.Exp, [stk], [T_PT[b]], scale=scale)
                if diag:
                    P.op("dve", lambda e, b=b, c0=c0: e.memset(PT[b][64:128, c0:c0 + 64], 0.0), reads=[], writes=[T_PT[b]])

            def issue_PV(g):
                hi, h, idx = items[g]
                j, nk, c0, diag = keytiles[idx]
                b = g % 4
                ob, ot = psb[4 + hi % 2], ps_tk[4 + hi % 2]
                db, dt_ = psb[6 + hi % 2], ps_tk[6 + hi % 2]
                MM(ob[:, c0:nq], V[0:nk, j, h * 128:(h + 1) * 128], PT[b][0:nk, c0:nq], idx == 0, idx == nk_t - 1,
                   [T_V, T_PT[b]], [ot])
                acc, tacc = SC[hi % 2], T_SC[hi % 2]
                if idx == 0:
                    CP("dve", acc[0:nk, c0:nq], PT[b][0:nk, c0:nq], [T_PT[b]], [tacc])
                else:
                    TT("dve", acc[0:nk, c0:nq], acc[0:nk, c0:nq], PT[b][0:nk, c0:nq], ALU.add, [tacc, T_PT[b]], [tacc])
                if idx == nk_t - 1:
                    accb = cq_nT[:, 0, :]
                    CP("dve", accb[:, 0:nq], acc[:, 0:nq], [tacc], [T_cqnT])
                    MM(db[:, 0:nq], ones[:, :], accb[:, 0:nq], True, True, [T_ones, T_cqnT], [dt_])
                    sc, tsc = SC[hi % 2], T_SC[hi % 2]
                    P.op("dve", lambda e, sc=sc, db=db: e.reciprocal(out=sc[:, 0:nq], in_=db[:, 0:nq]), reads=[dt_], writes=[tsc])
                    o_ap, o_tk = out_of(h)
                    TT("dve", o_ap, ob[:, 0:nq], sc[:, 0:nq], ALU.mult, [ot, tsc], [o_tk])
            LA = 3
            n_it = len(items)
            for g in range(min(LA, n_it)):
                issue_S(g)
            for g in range(n_it):
                issue_PV(g)
                if g + LA < n_it:
                    issue_S(g + LA)

        def mem_attention(qm, T_qm, q0, nq, out, T_out):
            items = [(hm, mt) for hm in range(4) for mt in range(2)]

            def bk(i):
                return psb[i], ps_tk[i]

            def issue_S(i):
                hm, mt = items[i]
                sbk, stk = bk(i % 2)
                for dc in range(2):
                    MM(sbk[:, 0:nq], KmT[:, hm * 2 + dc, mt * 128:(mt + 1) * 128], qm[:, hm * 2 + dc, q0:q0 + nq],
                       dc == 0, dc == 1, [T_KmT, T_qm], [stk])
                ACT(PT[i % 4][:, 0:nq], sbk[:, 0:nq], AF.Exp, [stk], [T_PT[i % 4]], scale=MEM_SCALE)

            def issue_PV(i):
                hm, mt = items[i]
                p = hm % 2
                b = i % 4
                for mc in range(2):
                    ob, ot = bk(2 + 3 * p + mc)
                    MM(ob[:, 0:nq], Vm[:, mt, hm * 256 + mc * 128:hm * 256 + (mc + 1) * 128], PT[b][:, 0:nq],
                       mt == 0, mt == 1, [T_Vm, T_PT[b]], [ot])
                db, dt_ = bk(4 + 3 * p)
                MM(db[:, 0:nq], ones[:, :], PT[b][:, 0:nq], mt == 0, mt == 1, [T_ones, T_PT[b]], [dt_])
                if mt == 1:
                    sc, tsc = SC[p], T_SC[p]
                    P.op("dve", lambda e, sc=sc, db=db: e.reciprocal(out=sc[:, 0:nq], in_=db[:, 0:nq]), reads=[dt_], writes=[tsc])
                    for mc in range(2):
                        ob, ot = bk(2 + 3 * p + mc)
                        TT("dve", out[:, hm * 2 + mc, q0:q0 + nq], ob[:, 0:nq], sc[:, 0:nq], ALU.mult, [ot, tsc], [T_out])
            LA = 2
            for i in range(LA):
                issue_S(i)
            for i in range(len(items)):
                issue_PV(i)
                if i + LA < len(items):
                    issue_S(i + LA)

        def build_kv(wkv, srcT, T_src, c0, ncols, key0):
            for h in range(H):
                pb, pt = ps_next()
                for kc in range(2):
                    MM(pb[:, 0:ncols], wkv.ap[:, kc, h * 128:(h + 1) * 128], srcT[:, kc, c0:c0 + ncols], kc == 0, kc == 1,
                       [wkv.tk, T_src], [pt])
                CPrr(Kn[:, h, key0:key0 + ncols], pb[:, 0:ncols], [pt], [T_Kn])
            nkt = (ncols + 127) // 128
            for t_ in range(nkt):
                nk = min(128, ncols - t_ * 128)
                j = key0 // 128 + t_
                for nh in range(2):
                    pb, pt = ps_next()
                    for kc in range(2):
                        MM(pb[0:nk, :], srcT[:, kc, c0 + t_ * 128:c0 + t_ * 128 + nk], wkv.ap[:, 2 + kc, nh * 512:(nh + 1) * 512],
                           kc == 0, kc == 1, [wkv.tk, T_src], [pt])
                    CPrr(V[0:nk, j, nh * 512:(nh + 1) * 512], pb[0:nk, :], [pt], [T_V])

        sg_h = [SC[2][:].bitcast(BF16)[:, 0:512], SC[2][:].bitcast(BF16)[:, 512:1024]]
        tmp_h = [SC[3][:].bitcast(BF16)[:, 0:512], SC[3][:].bitcast(BF16)[:, 512:1024]]
        T_sg = [tk("sg0"), tk("sg1")]
        T_tmp = [tk("tmp0"), tk("tmp1")]

        def branch_merge(gname, oT, T_o, first, TTn):
            for j in range(4):
                wc = w_get(f"{gname}{j}")
                for ocl in range(2):
                    oc = 2 * j + ocl
                    pg, tg = ps_next()
                    for kc in range(8):
                        MM(pg[:, 0:TTn], wc.ap[:, kc, ocl * 128:(ocl + 1) * 128], xnT[:, kc, 0:TTn], kc == 0, kc == 7,
                           [wc.tk, T_xnT], [tg])
                    sg, tsg = sg_h[oc % 2], T_sg[oc % 2]
                    ACT(sg[:, 0:TTn], pg[:, 0:TTn], AF.Sigmoid, [tg], [tsg])
                    pbk, tb = ps_next()
                    for kc in range(8):
                        MM(pbk[:, 0:TTn], wc.ap[:, kc, 256 + ocl * 128:256 + (ocl + 1) * 128], oT[:, kc, 0:TTn], kc == 0, kc == 7,
                           [wc.tk, T_o], [tb])
                    if first:
                        TT("dve", mrg[:, oc, 0:TTn], pbk[:, 0:TTn], sg[:, 0:TTn], ALU.mult, [tb, tsg], [T_mrg if oc < 4 else T_mrgB])
                    else:
                        tm_, ttm = tmp_h[oc % 2], T_tmp[oc % 2]
                        TT("dve", tm_[:, 0:TTn], pbk[:, 0:TTn], sg[:, 0:TTn], ALU.mult, [tb, tsg], [ttm])
                        tmg = T_mrg if oc < 4 else T_mrgB
                        TT("pool", mrg[:, oc, 0:TTn], mrg[:, oc, 0:TTn], tm_[:, 0:TTn], ALU.add, [ttm, tmg], [tmg])
                w_rel(wc)

        mrgx = mrg_f

        def token_tile(nst, TS, x_rows, y_rows, ckv_rows, kr_rows, rq_src, rk_src, is_sample, key0, tile_i, gv_rows=None,
                       pre=False, nxt=None):
            TTn = nst * TS
            WsT_u = WsT_s if is_sample else WsT
            brow_u = b_row_s if is_sample else b_row
            uT, T_u = AR[:, 0], T_A[0]
            vn, T_vn = AR[:, 1], T_A[1]
            vn_tm = AR[:, 1].rearrange("p a b -> p (a b)").rearrange("p (s d) -> p s d", s=4)
            oc_, T_oc = AR[:, 2], T_A[2]
            Qn, T_Qn = AR[:, 0], T_A[0]
            Qr, T_Qr = AR[:, 1], T_A[1]
            Qm, T_Qm = AR[:, 0], T_A[0]
            npre = min(2, nst) if pre else 0
            for s_ in range(npre, nst):
                P.dma("sp", [(HB[0:TS, s_, :], x_rows(s_))], writes=[T_hb[s_]], sig=T_hb[s_])
            P.dma("sp", [(rq[:, 0, 0:TTn], rq_src[0]), (rq[:, 1, 0:TTn], rq_src[1])], writes=[T_rq], sig=T_rq)
            P.dma("sp", [(rk[0:TS, 0, 0:nst, :], rk_src[0]), (rk[0:TS, 1, 0:nst, :], rk_src[1])], writes=[T_rk], sig=T_rk)
            norm_transpose(nst, TS, [mrgx[0:TS, s_, :] if s_ < npre else HB[0:TS, s_, :] for s_ in range(nst)],
                           [T_mrg if s_ < npre else T_hb[s_] for s_ in range(nst)], 0, "x", xnT, T_xnT)
            stage("A")
            wv0 = w_get("v0"); wv1 = w_get("v1")
            ssv, t_ssv = stat("v", 4)
            for s_ in range(nst):
                gv, T_gv = HB[0:TS, 0, :], T_hb[0]
                for nh, wv in enumerate((wv0, wv1)):
                    pb, pt = ps_next()
                    for kc in range(8):
                        MM(pb[0:TS, :], xnT[:, kc, s_ * TS:(s_ + 1) * TS], wv.ap[:, kc, :], kc == 0, kc == 7, [wv.tk, T_xnT], [pt])
                    ACT(gv[:, nh * 512:(nh + 1) * 512], pb[0:TS, :], AF.Gelu_apprx_tanh, [pt], [T_gv])
                ACT(xn_st[s_ % 2][0:TS, :], gv, AF.Square, [T_gv], [T_xn[s_ % 2], t_ssv], scale=1.0 / 32, accum=ssv[0:TS, s_:s_ + 1])
                rstd(ssv[0:TS, s_:s_ + 1], 1, t_ssv)
                if is_sample:
                    gvo, T_gvo = HB[0:TS, 2, :], T_hb[2]
                    STT("dve", gvo, gv, ssv[0:TS, s_:s_ + 1], g_gm_bc[0:TS, :], ALU.mult, ALU.mult, [T_gv, t_ssv, T_gbc], [T_gvo])
                    out_ops.append(P.dma("act", [(gv_rows, gvo)], reads=[T_gvo], sig=T_gvo))
                    CP("act", vn_tm[0:TS, s_, :], gvo, [T_gvo], [T_vn])
                else:
                    STT("dve", vn_tm[0:TS, s_, :], gv, ssv[0:TS, s_:s_ + 1], g_gm_bc[0:TS, :], ALU.mult, ALU.mult,
                        [T_gv, t_ssv, T_gbc], [T_vn])
            w_rel(wv0); w_rel(wv1)
            for j in range(2):
                wu = w_get(f"u{j}")

                def ev_u(oc, pb, pt):
                    ACT(uT[:, oc, 0:TTn], pb[:, 0:TTn], AF.Gelu_apprx_tanh, [pt], [T_u])
                fm_proj(wu, [(ocl, 4 * j + ocl) for ocl in range(4)], xnT, T_xnT, 8, TTn, ev_u)
                w_rel(wu)
            for g in range(8):
                pb, pt = ps_next()
                for s_ in range(nst):
                    MM(pb[:, s_ * TS:(s_ + 1) * TS], ones[:, 0:128], brow_u[:, g, :], True, False, [T_ones, T_brow], [pt])
                    MM(pb[:, s_ * TS:(s_ + 1) * TS], vn_tm[0:TS, s_, g * 128:(g + 1) * 128], WsT_u[0:TS, g, :], False, True,
                       [T_vn, T_WsT], [pt])
                TT("dve", oc_[:, g, 0:TTn], pb[:, 0:TTn], uT[:, g, 0:TTn], ALU.mult, [pt, T_u], [T_oc])
            branch_merge("ga", oc_, T_oc, True, TTn)
            stage("B")
            issue_prep(3)
            wc0 = w_get("c0"); wc1 = w_get("c1")
            ssc, t_ssc = stat("c", 8)
            bQ0, bQ1, bK0 = ps_hold(), ps_hold(), ps_hold()
            def c_stage1(s_):
                tm, T_tm = HB[0:TS, s_, 0:704], T_hb[s_]
                for jj, wc in enumerate((wc0, wc1)):
                    pb, pt = ps_next()
                    for kc in range(8):
                        MM(pb[0:TS, 0:352], xnT[:, kc, s_ * TS:(s_ + 1) * TS], wc.ap[:, kc, :], kc == 0, kc == 7, [wc.tk, T_xnT], [pt])
                    CPrr(tm[:, jj * 352:(jj + 1) * 352], pb[0:TS, 0:352], [pt], [T_tm])

            def c_stage2(s_):
                tm, T_tm = HB[0:TS, s_, 0:704], T_hb[s_]
                cq_n, T_cqn = cq_n2[s_ % 2], T_cqn2[s_ % 2]
                ckv_nb, T_ckvnb = ckv_nb2[s_ % 2], T_ckvnb2[s_ % 2]
                kr_b, T_krb = kr_b2[s_ % 2], T_krb2[s_ % 2]
                jk, T_jk = xn_st[s_ % 2], T_xn[s_ % 2]
                ssq = ssc[0:TS, 2 * s_:2 * s_ + 2]
                ACT(jk[0:TS, 0:QL], tm[:, 0:QL], AF.Square, [T_tm], [T_jk, t_ssc], scale=float(QL ** -0.5), accum=ssq[:, 0:1])
                ACT(jk[0:TS, 0:KVL], tm[:, QL:QL + KVL], AF.Square, [T_tm], [T_jk, t_ssc], scale=float(KVL ** -0.5), accum=ssq[:, 1:2])
                rstd(ssq, 2, t_ssc)
                TSC("dve", cq_n[0:TS, :], tm[:, 0:QL], ssq[:, 0:1], None, ALU.mult, None, [T_tm, t_ssc], [T_cqn])
                co, T_co = ckv_o[s_ % 2], T_ckvo[s_ % 2]
                STT("dve", co[0:TS, :], tm[:, QL:QL + KVL], ssq[:, 1:2], g_kv_bc[0:TS, :], ALU.mult, ALU.mult,
                    [T_tm, t_ssc, T_gbc], [T_co])
                out_ops.append(P.dma("act", [(ckv_rows(s_), co[0:TS, :])], reads=[T_co], sig=T_co))
                CP("act", ckv_nb[0:TS, :], co[0:TS, :], [T_co], [T_ckvnb])
                xk = tm[:, 640:704]
                ko, T_ko = kr_o[s_ % 2], T_kro[s_ % 2]
                TT("dve", rope_t[0:TS, 0, :], xk, rk[0:TS, 0, s_, :], ALU.mult, [T_tm, T_rk], [T_ropet])
                TT("dve", rope_t[0:TS, 1, :], xk, rk[0:TS, 1, s_, :], ALU.mult, [T_tm, T_rk], [T_ropet])
                TT("dve", ko[0:TS, 0:32], rope_t[0:TS, 0, 0:32], rope_t[0:TS, 1, 32:64], ALU.subtract, [T_ropet], [T_ko])
                TT("dve", ko[0:TS, 32:64], rope_t[0:TS, 1, 0:32], rope_t[0:TS, 0, 32:64], ALU.add, [T_ropet], [T_ko])
                out_ops.append(P.dma("act", [(kr_rows(s_), ko[0:TS, :])], reads=[T_ko], sig=T_ko))
                CP("act", kr_b[0:TS, 0:64], ko[0:TS, :], [T_ko], [T_krb])
                CP("act", kr_b[0:TS, 64:128], ko[0:TS, :], [T_ko], [T_krb])

            def c_stage3(s_):
                cq_n, T_cqn = cq_n2[s_ % 2], T_cqn2[s_ % 2]
                ckv_nb, T_ckvnb = ckv_nb2[s_ % 2], T_ckvnb2[s_ % 2]
                kr_b, T_krb = kr_b2[s_ % 2], T_krb2[s_ % 2]
                for kc in range(3):
                    pb, pt = (bQ0, bQ1)[kc // 2]
                    o = (kc % 2) * 512 + s_ * TS
                    TR(pb[:].bitcast(BF16)[:, o:o + TS], cq_n[0:TS, kc * 128:(kc + 1) * 128], TS, [T_cqn], [pt])
                for kc in range(2):
                    o = kc * 512 + s_ * TS
                    TR(bK0[0][:].bitcast(BF16)[:, o:o + TS], ckv_nb[0:TS, kc * 128:(kc + 1) * 128], TS, [T_ckvnb], [bK0[1]])
                o = 512 + s_ * TS
                TR(bQ1[0][:].bitcast(BF16)[:, o:o + TS], kr_b[0:TS, :], TS, [T_krb], [bQ1[1]])

            if nst == 4:
                c_stage1(0); c_stage1(1); c_stage2(0); c_stage1(2); c_stage3(0); c_stage2(1)
                c_stage1(3); c_stage3(1); c_stage2(2); c_stage3(2); c_stage2(3); c_stage3(3)
            else:
                for s_ in range(nst):
                    c_stage1(s_)
                    c_stage2(s_)
                    c_stage3(s_)
            w_rel(wc0); w_rel(wc1)
            ps_rel(bQ0, bQ1, bK0)
            for kc in range(3):
                pb, pt = (bQ0, bQ1)[kc // 2]
                o = (kc % 2) * 512
                TSC("dve", cq_nT[:, kc, 0:TTn], pb[:].bitcast(BF16)[:, o:o + TTn], gcol[:, 24 + kc:25 + kc], None, ALU.mult, None,
                    [pt, T_gcol], [T_cqnT])
            for kc in range(2):
                CP("act", ckv_nT[:, kc, 0:TTn], bK0[0][:].bitcast(BF16)[:, kc * 512:kc * 512 + TTn], [bK0[1]], [T_ckvnT])
            if is_sample:
                CP("act", krT_t[:, 0:TTn], bQ1[0][:].bitcast(BF16)[:, 512:512 + TTn], [bQ1[1]], [T_krTt])
            else:
                CP("act", KrT[:, key0:key0 + TTn], bQ1[0][:].bitcast(BF16)[:, 512:512 + TTn], [bQ1[1]], [T_KrT])
            wq0 = w_get("q0")

            def ev_qn(h, pb, pt):
                CPrr(Qn[:, h, 0:TTn], pb[:, 0:TTn], [pt], [T_Qn])
            fm_proj(wq0, [(h, h) for h in range(H)], cq_nT, T_cqnT, 3, TTn, ev_qn)
            w_rel(wq0)
            wq1 = w_get("q1")
            for hp in range(4):
                px, tx = ps_next()
                pxs, txs = ps_next()
                for kc in range(3):
                    MM(px[:, 0:TTn], wq1.ap[:, kc, hp * 128:(hp + 1) * 128], cq_nT[:, kc, 0:TTn], kc == 0, kc == 2, [wq1.tk, T_cqnT], [tx])
                for kc in range(3):
                    MM(pxs[:, 0:TTn], wq1.ap[:, kc, 512 + hp * 128:512 + (hp + 1) * 128], cq_nT[:, kc, 0:TTn], kc == 0, kc == 2,
                       [wq1.tk, T_cqnT], [txs])
                TT("dve", SC[0][:, 0:TTn], px[:, 0:TTn], rq[:, 0, 0:TTn], ALU.mult, [tx, T_rq], [T_SC[0]])
                TT("dve", SC[1][:, 0:TTn], pxs[:, 0:TTn], rq[:, 1, 0:TTn], ALU.mult, [txs, T_rq], [T_SC[1]])
                TT("pool", Qr[0:64, 2 * hp, 0:TTn], SC[0][0:64, 0:TTn], SC[1][0:64, 0:TTn], ALU.add, [T_SC[0], T_SC[1]], [T_Qr])
                TT("pool", Qr[64:128, 2 * hp + 1, 0:TTn], SC[0][64:128, 0:TTn], SC[1][64:128, 0:TTn], ALU.add, [T_SC[0], T_SC[1]], [T_Qr])
                P.op("pool", lambda e, hp=hp: e.memset(Qr[64:128, 2 * hp, 0:TTn], 0.0), reads=[], writes=[T_Qr])
                P.op("pool", lambda e, hp=hp: e.memset(Qr[0:64, 2 * hp + 1, 0:TTn], 0.0), reads=[], writes=[T_Qr])
            w_rel(wq1)
            stage("C")
            issue_prep(4)
            wkv = w_get("kv")
            if not is_sample:
                build_kv(wkv, ckv_nT, T_ckvnT, 0, TTn, key0)
                w_rel(wkv)
                kts = []
                for j in range(4 * tile_i):
                    kts.append((j, 128, 0, False))
                for r in range(4):
                    kts.append((4 * tile_i + r, 128, 128 * r, True))
                attention(list(range(H)), lambda h: (Qn[:, h, :], T_Qn), lambda h: (Qr[:, h, :], T_Qr, 0),
                          TTn, kts, lambda h: (oc_[:, h, 0:TTn], T_oc), MLA_SCALE, "mla")
            else:
                for s in range(NS):
                    hb23 = HB[:, 2:4, :].rearrange("p a (t c) -> p (a t) c", c=KVL)
                    P.dma("sp", [(hb23[:, 0:NKT_S, :], c_ckv[s].rearrange("(t p) c -> p t c", p=128))],
                          writes=[T_hb[2], T_hb[3]], sig=T_hb[2])
                    cpb = HB[:, 0, :].bitcast(BF16).rearrange("p (t c) -> p t c", c=KVL)
                    cT = HB[:, 1, :].bitcast(BF16).rearrange("p (k n) -> p k n", k=2)
                    CP("dve", cpb[:, 0:NKT_S, :], hb23[:, 0:NKT_S, :], [T_hb[2], T_hb[3]], [T_hb[0]])
                    for t_ in range(NKT_S):
                        for kc in range(2):
                            if (t_ * 2 + kc) % 8 == 0:
                                pb, pt = ps_next()
                            o = ((t_ * 2 + kc) % 8) * 128
                            TR(pb[:].bitcast(BF16)[:, o:o + 128], cpb[:, t_, kc * 128:(kc + 1) * 128], 128, [T_hb[0]], [pt])
                            if (t_ * 2 + kc) % 8 == 7:
                                for tt in range(4):
                                    for k2 in range(2):
                                        oo = (tt * 2 + k2) * 128
                                        tg = t_ - 3 + tt
                                        CPrr(cT[:, k2, tg * 128:(tg + 1) * 128], pb[:].bitcast(BF16)[:, oo:oo + 128], [pt], [T_hb[1]])
                    for cb in range(PAST // 512):
                        build_kv(wkv, cT, T_hb[1], cb * 512, 512, cb * 512)
                    build_kv(wkv, ckv_nT, T_ckvnT, DEC * s, DEC, PAST)
                    krl = HB[:, 2, :].rearrange("p (t c) -> p t c", c=ROPE)
                    P.dma("sp", [(krl[:, 0:NKT_S, :], c_kr[s].rearrange("(t p) c -> p t c", p=128))], writes=[T_hb[2]], sig=T_hb[2])
                    kpb = xn_st[0][:].rearrange("p (t c) -> p t c", c=128)
                    CP("dve", kpb[:, 0:NKT_S, 0:64], krl[:, 0:NKT_S, :], [T_hb[2]], [T_xn[0]])
                    CP("act", kpb[:, 0:NKT_S, 64:128], krl[:, 0:NKT_S, :], [T_hb[2]], [T_xn[0]])
                    for t_ in range(NKT_S):
                        if t_ % 8 == 0:
                            pb, pt = ps_next()
                        o = (t_ % 8) * 128
                        TR(pb[:].bitcast(BF16)[:, o:o + 128], kpb[:, t_, :], 128, [T_xn[0]], [pt])
                        if t_ % 8 == 7 or t_ == NKT_S - 1:
                            n_ = (t_ % 8) + 1
                            t0_ = t_ - (t_ % 8)
                            CPrr(KrT[:, t0_ * 128:(t0_ + n_) * 128], pb[:].bitcast(BF16)[:, 0:n_ * 128], [pt], [T_KrT])
                    CP("act", KrT[:, PAST:PAST + DEC], krT_t[:, DEC * s:DEC * (s + 1)], [T_krTt], [T_KrT])
                    kts = [(j, 128, 0, False) for j in range(NKT_S)] + [(NKT_S, DEC, 0, False)]
                    attention(list(range(H)),
                              lambda h, s=s: (Qn[:, h, DEC * s:DEC * (s + 1)], T_Qn),
                              lambda h, s=s: (Qr[:, h, DEC * s:DEC * (s + 1)], T_Qr, 0),
                              DEC, kts, lambda h, s=s: (oc_[:, h, DEC * s:DEC * (s + 1)], T_oc), MLA_SCALE, "mla")
                w_rel(wkv)
            stage("D")
            issue_prep(5)
            branch_merge("gb", oc_, T_oc, False, TTn)
            for j in range(2):
                wq = w_get(f"qm{j}")

                def ev_qm(oc, pb, pt):
                    CPrr(Qm[:, oc, 0:TTn], pb[:, 0:TTn], [pt], [T_Qm])
                fm_proj(wq, [(ocl, 4 * j + ocl) for ocl in range(4)], xnT, T_xnT, 8, TTn, ev_qm)
                w_rel(wq)
            if not is_sample:
                mem_attention(Qm, T_Qm, 0, TTn, oc_, T_oc)
            else:
                for s in range(NS):
                    load_mem_cache(s)
                    mem_attention(Qm, T_Qm, DEC * s, DEC, oc_, T_oc)
            branch_merge("gc", oc_, T_oc, False, TTn)
            stage("E")
            for s_ in range(nst):
                P.dma("sp", [(HB[0:TS, s_, :], x_rows(s_))], writes=[T_hb[s_]], sig=T_hb[s_])
            wo2 = [w_get("o0"), w_get("o1")]
            ssh, t_ssh = stat("h", 4)
            hbanks = [ps_hold() for _ in range(4)]

            def g_mm(s_):
                for nh in range(2):
                    wo = wo2[nh]
                    pb, pt = ps_next()
                    for kc in range(8):
                        MM(pb[0:TS, :], mrg[:, kc, s_ * TS:(s_ + 1) * TS], wo.ap[:, kc, :], kc == 0, kc == 7,
                           [wo.tk, T_mrg if kc < 4 else T_mrgB], [pt])
                    hsl = HB[0:TS, s_, nh * 512:(nh + 1) * 512]
                    TT("dve", hsl, pb[0:TS, :], hsl, ALU.add, [pt, T_hb[s_]], [T_hb[s_]])

            def g_norm(s_):
                src, tsrc = HB[0:TS, s_, :], T_hb[s_]
                xb, txb = xn_st[s_ % 2], T_xn[s_ % 2]
                ACT(xb[0:TS, :], src, AF.Square, [tsrc], [txb, t_ssh], scale=1.0 / 32, accum=ssh[0:TS, s_:s_ + 1])
                rstd(ssh[0:TS, s_:s_ + 1], 1, t_ssh)
                TSC("dve", xb[0:TS, :], src, ssh[0:TS, s_:s_ + 1], None, ALU.mult, None, [tsrc, t_ssh], [txb])

            def g_tr(s_):
                xb, txb = xn_st[s_ % 2], T_xn[s_ % 2]
                for kc in range(8):
                    pb, pt = hbanks[kc // 2]
                    o = (kc % 2) * 512 + s_ * TS
                    TR(pb[:].bitcast(BF16)[:, o:o + TS], xb[0:TS, kc * 128:(kc + 1) * 128], TS, [txb], [pt])

            for s_ in range(nst):
                g_mm(s_)
                g_norm(s_)
                if s_ >= 1:
                    g_tr(s_ - 1)
            g_tr(nst - 1)
            w_rel(wo2[0]); w_rel(wo2[1])
            ps_rel(*hbanks)
            stage("G")
            for kc in range(8):
                pb, pt = hbanks[kc // 2]
                pbb = pb[:].bitcast(BF16)
                o = (kc % 2) * 512
                dst = xnT[:, kc, 0:TTn]
                if kc % 2 == 0:
                    TSC("dve", dst, pbb[:, o:o + TTn], gcol[:, 8 + kc:8 + kc + 1], None, ALU.mult, None, [pt, T_gcol], [T_xnT])
                else:
                    ACT(dst, pbb[:, o:o + TTn], AF.Copy, [pt, T_gcol], [T_xnT], scale=gcol[:, 8 + kc:8 + kc + 1])
            aT = AR[:].rearrange("p a b c -> p (a b) c")
            for j in range(11):
                wf = w_get(f"f{j}")
                for ocl in range(2):
                    oc = 2 * j + ocl
                    pg, tg = ps_next()
                    pu, tu = ps_next()
                    for kc in range(8):
                        MM(pg[:, 0:TTn], wf.ap[:, kc, ocl * 128:(ocl + 1) * 128], xnT[:, kc, 0:TTn], kc == 0, kc == 7, [wf.tk, T_xnT], [tg])
                    for kc in range(8):
                        MM(pu[:, 0:TTn], wf.ap[:, kc, 256 + ocl * 128:256 + (ocl + 1) * 128], xnT[:, kc, 0:TTn], kc == 0, kc == 7,
                           [wf.tk, T_xnT], [tu])
                    sgb, tsg = SC[oc % 2], T_SC[oc % 2]
                    sg = sgb[:].bitcast(BF16)
                    ACT(sg[:, 0:TTn], pg[:, 0:TTn], AF.Silu, [tg], [tsg])
                    TT("dve", aT[:, oc, 0:TTn], pu[:, 0:TTn], sg[:, 0:TTn], ALU.mult, [tu, tsg], [T_A[oc // 8]])
                w_rel(wf)
            if nxt is not None:
                n_nst, n_TS, n_rows = nxt
                P.dma("sp", [(mrgx[0:n_TS, s_, :], n_rows(s_)) for s_ in range(min(2, n_nst))], writes=[T_mrg], sig=T_mrg)
            for nh in range(2):
                banks = [ps_hold() for _ in range(nst)]
                for kg in range(3):
                    wd = w_get(f"fd{nh}{kg}")
                    for s_ in range(nst):
                        pb, pt = banks[s_]
                        for kcl in range(KG[kg]):
                            kc = kg * 8 + kcl
                            MM(pb[0:TS, :], aT[:, kc, s_ * TS:(s_ + 1) * TS], wd.ap[:, kcl, :], kc == 0, kc == 21,
                               [wd.tk, T_A[kc // 8]], [pt])
                    w_rel(wd)
                ps_rel(*banks)
                for s_ in range(nst):
                    pb, pt = banks[s_]
                    hsl = HB[0:TS, s_, nh * 512:(nh + 1) * 512]
                    TT("dve", hsl, pb[0:TS, :], hsl, ALU.add, [pt, T_hb[s_]], [T_hb[s_]])
            stage("H")
            ssf, t_ssf = stat("f", 4)
            T_ys = [T_mrg, T_mrgB]
            for s_ in range(nst):
                hs = HB[0:TS, s_, :]
                ys, tys = mrg_f[0:TS, s_ % 2, :], T_ys[s_ % 2]
                ACT(xn_st[s_ % 2][0:TS, :], hs, AF.Square, [T_hb[s_]], [T_xn[s_ % 2], t_ssf], scale=1.0 / 32, accum=ssf[0:TS, s_:s_ + 1])
                rstd(ssf[0:TS, s_:s_ + 1], 1, t_ssf)
                STT("dve", ys, hs, ssf[0:TS, s_:s_ + 1], g_fin_bc[0:TS, :], ALU.mult, ALU.mult, [T_hb[s_], t_ssf, T_gbc], [tys])
                out_ops.append(P.dma("act", [(y_rows(s_), ys)], reads=[tys], sig=tys))

        def kmT_from_tokmajor(kb, T_kb, mt):
            for half in range(2):
                pb, pt = ps_next()
                for q in range(4):
                    hd = half * 4 + q
                    TR(pb[:].bitcast(BF16)[:, q * 128:(q + 1) * 128], kb[:, hd * 128:(hd + 1) * 128], 128, [T_kb], [pt])
                for q in range(4):
                    hd = half * 4 + q
                    CPrr(KmT[:, hd, mt * 128:(mt + 1) * 128], pb[:].bitcast(BF16)[:, q * 128:(q + 1) * 128], [pt], [T_KmT])

        def load_mem_cache(s):
            P.dma("sp", [(HB[:, 2, :], c_mk[s, 0:128, :]), (HB[:, 3, :], c_mk[s, 128:256, :])],
                  writes=[T_hb[2], T_hb[3]], sig=T_hb[2])
            for mt in range(2):
                CP("dve" if mt else "act", xn_st[mt][:], HB[:, 2 + mt, :], [T_hb[2], T_hb[3]], [T_xn[mt]])
                kmT_from_tokmajor(xn_st[mt], T_xn[mt], mt)
            P.dma("sp", [(HB[:, 2, :], c_mv[s, 0:128, :]), (HB[:, 3, :], c_mv[s, 128:256, :])],
                  writes=[T_hb[2], T_hb[3]], sig=T_hb[2])
            for mt in range(2):
                CP("dve" if mt else "act", Vm[:, mt, :], HB[:, 2 + mt, :], [T_hb[2], T_hb[3]], [T_Vm])

        def prompt_mem_kv(b):
            P.dma("sp", [(HB[:, 0, :], mem_p[b, 0:128, :]), (HB[:, 1, :], mem_p[b, 128:256, :])],
                  writes=[T_hb[0], T_hb[1]], sig=T_hb[0])
            stage("mem_load")
            norm_transpose(2, 128, [HB[:, 0, :], HB[:, 1, :]], [T_hb[0], T_hb[1]], 16, "m", xnT, T_xnT)
            stage("mem_norm")
            for c in range(4):
                wm = w_get(f"m{c}")
                kv, half = c // 2, c % 2
                for mt in range(2):
                    pb, pt = ps_next()
                    for kc in range(8):
                        MM(pb[:, :], xnT[:, kc, mt * 128:(mt + 1) * 128], wm.ap[:, kc, :], kc == 0, kc == 7, [wm.tk, T_xnT], [pt])
                    CPrr(HB[:, 2 + mt, half * 512:(half + 1) * 512], pb[:, :], [pt], [T_hb[2 + mt]])
                w_rel(wm)
                if half == 1:
                    dst = mk_p if kv == 0 else mv_p
                    for mt in range(2):
                        out_ops.append(P.dma("act", [(dst[b * NMEM + mt * 128:b * NMEM + (mt + 1) * 128, :], HB[:, 2 + mt, :])],
                                             reads=[T_hb[2 + mt]], sig=T_hb[2 + mt]))
                        if kv == 0:
                            CP("act", xn_st[mt][:], HB[:, 2 + mt, :], [T_hb[2 + mt]], [T_xn[mt]])
                            kmT_from_tokmajor(xn_st[mt], T_xn[mt], mt)
                        else:
                            CP("act", Vm[:, mt, :], HB[:, 2 + mt, :], [T_hb[2 + mt]], [T_Vm])

        wstate["released"] = 0
        try:
          stage("consts")
          w_load_next_if_room()
          stage("wload")
          tiles = []
          for b in range(NP):
            for ti in range(NT):
                r0 = b * SEQ + ti * 512
                tiles.append(dict(b=b, ti=ti, r0=r0, nst=4, TS=128,
                                  x_rows=(lambda s_, r0=r0: x_p[r0 + s_ * 128:r0 + (s_ + 1) * 128, :])))
          tiles.append(dict(b=-1, ti=0, r0=0, nst=1, TS=TS_S, x_rows=(lambda s_: x_s[:, :])))
          for k, td in enumerate(tiles):
            nx = tiles[k + 1] if k + 1 < len(tiles) else None
            nxt = (nx["nst"], nx["TS"], nx["x_rows"]) if nx is not None else None
            pre = (k > 0) and bool(_os.environ.get("KPRE"))
            nxt = nxt if _os.environ.get("KPRE") else None
            if td["b"] >= 0:
                b, ti, r0 = td["b"], td["ti"], td["r0"]
                if ti == 0:
                    prompt_mem_kv(b)
                    stage("memkv")
                    issue_prep(2)
                token_tile(
                    4, 128, td["x_rows"],
                    lambda s_, r0=r0: y_p[r0 + s_ * 128:r0 + (s_ + 1) * 128, :],
                    lambda s_, r0=r0: ckv_p[r0 + s_ * 128:r0 + (s_ + 1) * 128, :],
                    lambda s_, r0=r0: kr_p[r0 + s_ * 128:r0 + (s_ + 1) * 128, :],
                    (rq_p[0, :, ti * 512:(ti + 1) * 512], rq_p[1, :, ti * 512:(ti + 1) * 512]),
                    (rk_p[0, ti * 512:(ti + 1) * 512, :].rearrange("(s p) c -> p s c", p=128),
                     rk_p[1, ti * 512:(ti + 1) * 512, :].rearrange("(s p) c -> p s c", p=128)),
                    False, ti * 512, ti, pre=pre, nxt=nxt)
            else:
                token_tile(
                    1, TS_S, td["x_rows"], lambda s_: y_s[:, :], lambda s_: ckv_s[:, :], lambda s_: kr_s[:, :],
                    (rq_s[0], rq_s[1]),
                    (rk_s[0].rearrange("(s p) c -> p s c", s=1), rk_s[1].rearrange("(s p) c -> p s c", s=1)),
                    True, 0, 0, gv_rows=gv_s[:, :], pre=pre, nxt=None)
          assert wstate["next_get"] == len(wseq)
        except _Stop:
            pass
        if _os.environ.get("KDUMP"):
            with open(_os.environ["KDUMP"], "w") as fh:
                for e in ENGS:
                    for i, o in enumerate(P.ops[e]):
                        fh.write(f"{e} {i} {o.label} {int(o.signal)} {len(o.deps)}\n")
        P.finalize(out_ops)
    return nc


def _rope_tables(pos):
    inv = (np.float32(10000.0) ** (-np.arange(0, ROPE, 2, dtype=np.float32) / np.float32(ROPE))).astype(np.float32)
    ang = pos.astype(np.float32)[:, None] * inv[None, :]
    cos, sin = np.cos(ang).astype(np.float32), np.sin(ang).astype(np.float32)
    cos64 = np.concatenate([cos, cos], axis=1)
    sin64 = np.concatenate([sin, sin], axis=1)
    sgn = np.concatenate([-np.ones(32, np.float32), np.ones(32, np.float32)])
    rq = np.stack([np.tile(cos64.T, (2, 1)), np.tile((sin64 * sgn[None, :]).T, (2, 1))]).astype(np.float32)
    rk = np.stack([cos64, sin64]).astype(np.float32)
    return np.ascontiguousarray(rq), np.ascontiguousarray(rk)


def make_in_maps(inp, n_cores, NP, SEQ, NS, PAST):
    f = lambda a: np.ascontiguousarray(np.asarray(a, dtype=np.float32))
    rq_p, rk_p = _rope_tables(np.arange(SEQ))
    rq_s, rk_s = _rope_tables(PAST + (np.arange(NS * DEC) % DEC))
    qi, pi = np.meshgrid(np.arange(128), np.arange(128), indexing="ij")
    shared = {
        "g_mix": f(inp["norm_mix_g"][0]), "g_gm": f(inp["gm_norm_g"][0]), "g_q": f(inp["mla_q_norm_g"][0]),
        "g_kv": f(inp["mla_kv_norm_g"][0]), "g_mem": f(inp["mem_norm_g"][0]), "g_ffn": f(inp["norm_ffn_g"][0]),
        "g_fin": f(inp["final_norm_g"]), "w_in": f(inp["w_in"][0]), "gm_ws": f(inp["gm_ws"][0]), "gm_bs": f(inp["gm_bs"][0]),
        "w_uq": f(np.asarray(inp["mla_w_uq"][0]).reshape(QL, H * 192)), "w_uk": f(np.asarray(inp["mla_w_uk"][0]).reshape(KVL, D)),
        "w_uv": f(np.asarray(inp["mla_w_uv"][0]).reshape(KVL, D)), "w_mkv": f(inp["mem_w_kv"][0]),
        "w_bgm": f(inp["w_br_gm"][0]), "w_bmla": f(inp["w_br_mla"][0]), "w_bmem": f(inp["w_br_mem"][0]), "w_o": f(inp["w_out"][0]),
        "w_g": f(inp["ffn_w_gate"][0]), "w_u": f(inp["ffn_w_up"][0]), "w_d": f(inp["ffn_w_down"][0]),
        "ident": np.eye(128, dtype=np.float32), "maskT": (qi <= pi).astype(np.float32),
        "rq_p": rq_p, "rk_p": rk_p, "rq_s": rq_s, "rk_s": rk_s,
    }
    maps = []
    for c in range(n_cores):
        m = dict(shared)
        m["x_p"] = f(np.asarray(inp["x_prompt"])[c * NP:(c + 1) * NP].reshape(NP * SEQ, D))
        m["x_s"] = f(np.asarray(inp["x_sample"])[c * NS:(c + 1) * NS].reshape(NS * DEC, D))
        m["c_ckv"] = f(np.asarray(inp["cache_mla_ckv"])[0, c * NS:(c + 1) * NS])
        m["c_kr"] = f(np.asarray(inp["cache_mla_krope"])[0, c * NS:(c + 1) * NS])
        m["c_mk"] = f(np.asarray(inp["cache_mem_k"])[0, c * NS:(c + 1) * NS].reshape(NS, NMEM, D))
        m["c_mv"] = f(np.asarray(inp["cache_mem_v"])[0, c * NS:(c + 1) * NS].reshape(NS, NMEM, D))
        m["mem_p"] = f(np.asarray(inp["mem_prompt"])[c * NP:(c + 1) * NP])
        maps.append(m)
    return maps


def gather_outputs(results, n_cores, NP, SEQ, NS):
    cat = lambda k: np.concatenate([np.asarray(r[k], dtype=np.float32) for r in results], axis=0)
    B, BS = n_cores * NP, n_cores * NS
    return (cat("y_p").reshape(B, SEQ, D), cat("y_s").reshape(BS, DEC, D),
            cat("ckv_p").reshape(1, B, SEQ, KVL), cat("kr_p").reshape(1, B, SEQ, ROPE),
            cat("mk_p").reshape(1, B, NMEM, 4, 256), cat("mv_p").reshape(1, B, NMEM, 4, 256),
            cat("ckv_s").reshape(1, BS, DEC, KVL), cat("kr_s").reshape(1, BS, DEC, ROPE),
            cat("gv_s").reshape(1, BS, DEC, D))


def kernel(**inputs):
    B, SEQ, _ = inputs["x_prompt"].shape
    BS = inputs["x_sample"].shape[0]
    PAST = inputs["cache_mla_ckv"].shape[2]
    NP, NS = B // N_CORES, BS // N_CORES
    nc = build_program(NP, SEQ, NS, PAST)
    maps = make_in_maps(inputs, N_CORES, NP, SEQ, NS, PAST)
    res = run_bass_kernel_spmd(nc, maps, core_ids=list(range(N_CORES)))
    return gather_outputs(res.results, N_CORES, NP, SEQ, NS)
```

```python
import numpy as np
import concourse.bass as bass
import concourse.mybir as mybir
from concourse.bass_utils import run_bass_kernel_spmd

F32 = mybir.dt.float32
BF16 = mybir.dt.bfloat16
AF = mybir.ActivationFunctionType
ALU = mybir.AluOpType
AX = mybir.AxisListType


class Tk:
    __slots__ = ("name", "w", "r", "sem", "semcnt")

    def __init__(self, name):
        self.name = name
        self.w = None
        self.r = []
        self.sem = None
        self.semcnt = 0


class Op:
    __slots__ = ("eng", "emit", "deps", "signal", "sigval", "sem", "is_dma", "nparts", "sig_tile", "idx", "label")

    def __init__(self, eng, emit, is_dma=False, nparts=1, sig_tile=None):
        self.eng = eng
        self.emit = emit
        self.deps = []
        self.signal = False
        self.sigval = None
        self.sem = None
        self.is_dma = is_dma
        self.nparts = nparts
        self.sig_tile = sig_tile


ENGS = ("pe", "act", "dve", "pool", "sp")


class Prog:
    def __init__(self, nc, same_engine_sync=True):
        self.nc = nc
        self.ops = {e: [] for e in ENGS}
        self.same_engine_sync = same_engine_sync
        self.final_ops = []
        self.n_tk = 0

    def tk(self, name=None):
        self.n_tk += 1
        return Tk(name or f"t{self.n_tk}")

    def _add(self, op, reads, writes):
        deps = []
        seen = set()
        for t in reads:
            if t.w is not None and id(t.w) not in seen:
                seen.add(id(t.w)); deps.append(t.w)
        for t in writes:
            if t.w is not None and id(t.w) not in seen:
                seen.add(id(t.w)); deps.append(t.w)
            for r in t.r:
                if id(r) not in seen:
                    seen.add(id(r)); deps.append(r)
        for d in deps:
            if d is op:
                continue
            if (not d.is_dma) and (not op.is_dma) and d.eng == op.eng:
                if d.eng == "pe":
                    continue
                if not self.same_engine_sync:
                    continue
            d.signal = True
            op.deps.append(d)
        for t in reads:
            if not op.is_dma:
                t.r = [r for r in t.r if r.is_dma or r.eng != op.eng]
            t.r.append(op)
        for t in writes:
            t.w = op
            t.r = []
        self.ops[op.eng].append(op)
        self.n_ops = getattr(self, "n_ops", 0) + 1
        op.idx = self.n_ops
        op.label = getattr(self, "label", "")
        return op

    def op(self, eng, emit, reads=(), writes=()):
        return self._add(Op(eng, emit), list(reads), list(writes))

    def dma(self, eng, pairs, reads=(), writes=(), sig=None):
        assert sig is not None

        def emit(e, pairs=pairs):
            return [e.dma_start(out=o, in_=i) for (o, i) in pairs]
        op = Op(eng, emit, is_dma=True, nparts=len(pairs), sig_tile=sig)
        op.signal = True
        return self._add(op, list(reads), list(writes))

    def finalize(self, final_wait_ops):
        nc = self.nc
        import contextlib
        with contextlib.ExitStack() as st:
            final_wait_ops = list(final_wait_ops)
            for e in ENGS:
                if self.ops[e]:
                    last = self.ops[e][-1]
                    last.signal = True
                    final_wait_ops.append(last)
                final_wait_ops += [o for o in self.ops[e] if o.is_dma]
            esem = {e: st.enter_context(nc.semaphore(f"S_{e}")) for e in ENGS if e != "sp"}
            cnt = {e: 0 for e in ENGS}
            tiles_with_sem = {}
            all_dma = sorted([o for e in ENGS for o in self.ops[e] if o.is_dma], key=lambda o: o.idx)
            for op in all_dma:
                t = op.sig_tile
                if t.sem is None:
                    t.sem = {}
                    t.semcnt = {}
                if op.eng not in t.sem:
                    t.sem[op.eng] = st.enter_context(nc.semaphore(f"D_{t.name}_{op.eng}"))
                    t.semcnt[op.eng] = 0
                    tiles_with_sem[(id(t), op.eng)] = t
                t.semcnt[op.eng] += 16 * op.nparts
                op.sem = t.sem[op.eng]
                op.sigval = t.semcnt[op.eng]
            for e in ENGS:
                for op in self.ops[e]:
                    if op.is_dma:
                        pass
                    elif op.signal:
                        cnt[e] += 1
                        op.sem = esem[e]
                        op.sigval = cnt[e]
            self.n_sems = len(tiles_with_sem) + 4
            block = st.enter_context(nc.Block())
            handles = {"pe": block.tensor, "act": block.scalar, "dve": block.vector,
                       "pool": block.gpsimd, "sp": block.sync}
            for e in ENGS:
                ops = self.ops[e]
                fin = final_wait_ops if e == "sp" else []

                def body(eng, ops=ops, fin=fin):
                    waited = {}

                    def do_waits(deps):
                        need = {}
                        for d in deps:
                            k = id(d.sem)
                            if k not in need or need[k][1] < d.sigval:
                                need[k] = (d.sem, d.sigval)
                        for k, (sem, v) in need.items():
                            if waited.get(k, 0) >= v:
                                continue
                            waited[k] = v
                            eng.wait_ge(sem, v)
                    for op in ops:
                        do_waits(op.deps)
                        r = op.emit(eng)
                        if op.is_dma:
                            for ins in r:
                                ins.then_inc(op.sem, 16)
                        elif op.signal:
                            r.then_inc(op.sem, 1)
                    do_waits(fin)
                handles[e](body)


D = 1024
DFF = 2816
QL, KVL, ROPE, NOPE, VD = 384, 256, 64, 128, 128
H = 8
NMEM = 256
EPS = 1e-6
MLA_SCALE = float((NOPE + ROPE) ** -0.5)
MEM_SCALE = float(256 ** -0.5)
DEC = 16
N_CORES = 8
NSLOT = 3

C_U, C_V, C_CQ, C_QM, C_GA, C_GB, C_GC = 0, 1024, 2048, 2752, 3776, 4800, 5824


def build_program(NP, SEQ, NS, PAST, dbg=False):
    import contextlib
    import os as _os
    nc = bass.Bass("TRN2", target_bir_lowering=False)
    NT = SEQ // 512
    NKT_S = PAST // 128
    TS_S = NS * DEC
    assert TS_S <= 128 and PAST % 512 == 0

    def din(name, shape, dt=F32):
        return nc.dram_tensor(name, list(shape), dt, kind="ExternalInput").ap()

    def dout(name, shape):
        return nc.dram_tensor(name, list(shape), F32, kind="ExternalOutput").ap()

    x_p = din("x_p", [NP * SEQ, D]); x_s = din("x_s", [TS_S, D])
    c_ckv = din("c_ckv", [NS, PAST, KVL]); c_kr = din("c_kr", [NS, PAST, ROPE])
    c_mk = din("c_mk", [NS, NMEM, D]); c_mv = din("c_mv", [NS, NMEM, D])
    mem_p = din("mem_p", [NP, NMEM, D])
    g_mix = din("g_mix", [D]); g_gm = din("g_gm", [D]); g_q = din("g_q", [QL]); g_kv = din("g_kv", [KVL])
    g_mem = din("g_mem", [D]); g_ffn = din("g_ffn", [D]); g_fin = din("g_fin", [D])
    w_in = din("w_in", [D, 6848]); gm_ws = din("gm_ws", [8, 128, 128]); gm_bs = din("gm_bs", [8, 128])
    w_uq = din("w_uq", [QL, H * 192]); w_uk = din("w_uk", [KVL, D]); w_uv = din("w_uv", [KVL, D])
    w_mkv = din("w_mkv", [D, 2 * D]); w_bgm = din("w_bgm", [D, D]); w_bmla = din("w_bmla", [D, D])
    w_bmem = din("w_bmem", [D, D]); w_o = din("w_o", [D, D])
    w_g = din("w_g", [D, DFF]); w_u = din("w_u", [D, DFF]); w_d = din("w_d", [DFF, D])
    ident_d = din("ident", [128, 128]); maskT_d = din("maskT", [128, 128])
    rq_p = din("rq_p", [2, 128, SEQ]); rk_p = din("rk_p", [2, SEQ, 64])
    rq_s = din("rq_s", [2, 128, TS_S]); rk_s = din("rk_s", [2, TS_S, 64])
    y_p = dout("y_p", [NP * SEQ, D]); y_s = dout("y_s", [TS_S, D])
    ckv_p = dout("ckv_p", [NP * SEQ, KVL]); kr_p = dout("kr_p", [NP * SEQ, ROPE])
    mk_p = dout("mk_p", [NP * NMEM, D]); mv_p = dout("mv_p", [NP * NMEM, D])
    ckv_s = dout("ckv_s", [TS_S, KVL]); kr_s = dout("kr_s", [TS_S, ROPE]); gv_s = dout("gv_s", [TS_S, D])

    st = contextlib.ExitStack()
    with st:
        P = Prog(nc)

        def sb(name, shape, dt):
            return st.enter_context(nc.sbuf_tensor("sb_" + name, list(shape), dt))

        chunks = {}
        prep_pairs = []

        def kview(src):
            return src.rearrange("(kc p) n -> p kc n", p=128)

        def mkchunk(name, KC, NC, pieces):
            t = nc.dram_tensor("ws_" + name, [128, KC * NC], BF16).ap()
            tv = t.rearrange("p (kc n) -> p kc n", kc=KC)
            for (kc0, kcn, c0, src) in pieces:
                n = src.shape[1]
                prep_pairs.append((name, tv[:, kc0:kc0 + kcn, c0:c0 + n], kview(src)))
            chunks[name] = (t, KC, NC)

        for j in range(2):
            mkchunk(f"v{j}", 8, 512, [(0, 8, 0, w_in[:, C_V + 512 * j:C_V + 512 * (j + 1)])])
            mkchunk(f"u{j}", 8, 512, [(0, 8, 0, w_in[:, C_U + 512 * j:C_U + 512 * (j + 1)])])
            mkchunk(f"c{j}", 8, 352, [(0, 8, 0, w_in[:, C_CQ + 352 * j:C_CQ + 352 * (j + 1)])])
            mkchunk(f"qm{j}", 8, 512, [(0, 8, 0, w_in[:, C_QM + 512 * j:C_QM + 512 * (j + 1)])])
            mkchunk(f"o{j}", 8, 512, [(0, 8, 0, w_o[:, 512 * j:512 * (j + 1)])])
        for j in range(4):
            for nm, cg, wb in (("ga", C_GA, w_bgm), ("gb", C_GB, w_bmla), ("gc", C_GC, w_bmem)):
                mkchunk(f"{nm}{j}", 8, 512, [(0, 8, 0, w_in[:, cg + 256 * j:cg + 256 * (j + 1)]),
                                             (0, 8, 256, wb[:, 256 * j:256 * (j + 1)])])
        mkchunk("q0", 3, 1024, [(0, 3, h * 128, w_uq[:, h * 192:h * 192 + 128]) for h in range(H)])
        pcs = []
        for h in range(H):
            b = h * 192 + 128
            pcs.append((0, 3, h * 64, w_uq[:, b:b + 64]))
            pcs.append((0, 3, 512 + h * 64, w_uq[:, b + 32:b + 64]))
            pcs.append((0, 3, 512 + h * 64 + 32, w_uq[:, b:b + 32]))
        mkchunk("q1", 3, 1024, pcs)
        mkchunk("kv", 4, 1024, [(0, 2, 0, w_uk), (2, 2, 0, w_uv)])
        for j in range(11):
            mkchunk(f"f{j}", 8, 512, [(0, 8, 0, w_g[:, 256 * j:256 * (j + 1)]), (0, 8, 256, w_u[:, 256 * j:256 * (j + 1)])])
        KG = [8, 8, 6]
        for nh in range(2):
            for kg in range(3):
                mkchunk(f"fd{nh}{kg}", KG[kg], 512,
                        [(0, KG[kg], 0, w_d[kg * 1024:kg * 1024 + KG[kg] * 128, 512 * nh:512 * (nh + 1)])])
        for j in range(4):
            mkchunk(f"m{j}", 8, 512, [(0, 8, 0, w_mkv[:, 512 * j:512 * (j + 1)])])

        TILE_ORDER = (["v0", "v1", "u0", "u1", "ga0", "ga1", "ga2", "ga3",
                       "c0", "c1", "q0", "q1", "kv", "gb0", "gb1", "gb2", "gb3",
                       "qm0", "qm1", "gc0", "gc1", "gc2", "gc3", "o0", "o1"]
                      + [f"f{j}" for j in range(11)]
                      + [f"fd{nh}{kg}" for nh in range(2) for kg in range(3)])
        wseq = []
        for b in range(NP):
            wseq += ["m0", "m1", "m2", "m3"] + TILE_ORDER * NT
        wseq += TILE_ORDER

        grp_lists = [["m0", "m1", "m2", "m3"],
                     ["v0", "v1", "u0", "u1", "ga0", "ga1", "ga2", "ga3"],
                     ["c0", "c1", "q0", "q1", "kv", "gb0", "gb1", "gb2", "gb3"],
                     ["qm0", "qm1", "gc0", "gc1", "gc2", "gc3", "o0", "o1"],
                     [f"f{j}" for j in range(11)],
                     [f"fd{nh}{kg}" for nh in range(2) for kg in range(3)]]
        chunk_grp = {nm: g for g, l in enumerate(grp_lists) for nm in l}
        assert set(chunk_grp) == set(chunks)
        T_prep_g = [P.tk(f"prep{g}") for g in range(len(grp_lists))]
        prep_issued = set()

        def issue_prep(g):
            if g in prep_issued:
                return
            prep_issued.add(g)
            prs = [(o, i) for (nm, o, i) in prep_pairs if chunk_grp[nm] == g]
            P.dma("pool", prs, writes=[T_prep_g[g]], sig=T_prep_g[g])
        issue_prep(0)
        issue_prep(1)
        wslot = [sb(f"wslot{i}", [128, 4096], BF16) for i in range(NSLOT)]
        wslot_tk = [P.tk(f"wslot{i}") for i in range(NSLOT)]
        wstate = {"next_load": 0, "next_get": 0}

        def w_load_next():
            i = wstate["next_load"]
            if i >= len(wseq):
                return
            wstate["next_load"] += 1
            t, KC, NC = chunks[wseq[i]]
            s = i % NSLOT
            assert chunk_grp[wseq[i]] in prep_issued, wseq[i]
            P.dma("sp", [(wslot[s][:, 0:KC * NC], t)], reads=[T_prep_g[chunk_grp[wseq[i]]]], writes=[wslot_tk[s]], sig=wslot_tk[s])

        class WC:
            pass

        def w_get(expect):
            i = wstate["next_get"]
            assert wseq[i] == expect, (wseq[i], expect)
            wstate["next_get"] += 1
            while wstate["next_load"] <= i:
                w_load_next()
            t, KC, NC = chunks[wseq[i]]
            s = i % NSLOT
            c = WC()
            c.ap = wslot[s][:, 0:KC * NC].rearrange("p (kc n) -> p kc n", kc=KC)
            c.tk = wslot_tk[s]
            c.KC, c.NC = KC, NC
            return c

        def w_load_next_if_room():
            while wstate["next_load"] < len(wseq) and wstate["next_load"] < wstate["released"] + NSLOT:
                w_load_next()
        wstate["released"] = 0

        def w_rel(c):
            wstate["released"] += 1
            w_load_next_if_room()

        psb = [st.enter_context(nc.psum_tensor(f"ps{i}", [128, 512], F32)) for i in range(8)]
        ps_tk = [P.tk(f"ps{i}") for i in range(8)]
        psstate = {"i": 0, "held": set()}
        ps_idx = {id(t): i for i, t in enumerate(ps_tk)}

        def ps_next():
            for _ in range(9):
                i = psstate["i"]
                psstate["i"] = (i + 1) % 8
                if i not in psstate["held"]:
                    return psb[i], ps_tk[i]
            raise RuntimeError("all PSUM banks held")

        def ps_hold():
            pb, pt = ps_next()
            psstate["held"].add(ps_idx[id(pt)])
            return pb, pt

        def ps_rel(*banks):
            for (pb, pt) in banks:
                psstate["held"].discard(ps_idx[id(pt)])

        ident_f = sb("ident_f", [128, 128], F32); ident = sb("ident", [128, 128], BF16)
        maskT = sb("maskT", [128, 128], F32)
        ones = sb("ones", [128, 128], BF16)
        mhalf = sb("mhalf", [128, 8], F32)
        g_gm_bc = sb("g_gm_bc", [128, D], F32); g_fin_bc = sb("g_fin_bc", [128, D], F32)
        g_kv_bc = sb("g_kv_bc", [128, KVL], F32)
        gcol = sb("gcol", [128, 32], F32)
        WsT = sb("WsT", [128, 8, 128], BF16); WsT_s = sb("WsT_s", [64, 8, 64], BF16)
        b_row = sb("b_row", [128, 8, 128], BF16); b_row_s = sb("b_row_s", [128, 8, 64], BF16)
        Kn = sb("Kn", [128, H, 2048], BF16); V = sb("V", [128, 16, D], BF16); KrT = sb("KrT", [128, 2048], BF16)
        KmT = sb("KmT", [128, 8, NMEM], BF16); Vm = sb("Vm", [128, 2, D], BF16)
        HB = sb("HB", [128, 4, D], F32)
        xnT = sb("xnT", [128, 8, 512], BF16)
        mrg_f = sb("mrg", [128, 2, D], F32)
        mrg = mrg_f[:].rearrange("p s d -> p (s d)").bitcast(BF16).rearrange("p (a b) -> p a b", a=8)
        AR = sb("AR", [128, 3, 8, 512], BF16)
        xn_st = [sb(f"xn_st{i}", [128, D], BF16) for i in range(2)]
        ckv_o = [sb(f"ckv_o{i}", [128, KVL], F32) for i in range(2)]
        kr_o = [sb(f"kr_o{i}", [128, ROPE], F32) for i in range(2)]
        cq_n2 = [sb(f"cq_n{i}", [128, QL], BF16) for i in range(2)]
        ckv_nb2 = [sb(f"ckv_nb{i}", [128, KVL], BF16) for i in range(2)]
        kr_b2 = [sb(f"kr_b{i}", [128, 128], BF16) for i in range(2)]
        rope_t = sb("rope_t", [128, 2, 64], F32)
        cq_nT = sb("cq_nT", [128, 3, 512], BF16); ckv_nT = sb("ckv_nT", [128, 2, 512], BF16)
        krT_t = sb("krT_t", [128, 512], BF16)
        rq = sb("rq", [128, 2, 512], F32); rk = sb("rk", [128, 2, 4, 64], F32)
        PT = [sb(f"PT{i}", [128, 512], BF16) for i in range(4)]
        SC = [sb(f"SC{i}", [128, 512], F32) for i in range(4)]
        stats = sb("stats", [128, 64], F32)
        cst = HB[:, 3, :]

        tk = P.tk
        T_ident, T_maskT, T_ones, T_mh = tk("ident"), tk("maskT"), tk("ones"), tk("mh")
        T_gbc, T_gcol, T_WsT, T_brow = tk("gbc"), tk("gcol"), tk("WsT"), tk("brow")
        T_Kn, T_V, T_KrT, T_KmT, T_Vm = tk("Kn"), tk("V"), tk("KrT"), tk("KmT"), tk("Vm")
        T_hb = [tk(f"hb{i}") for i in range(4)]
        T_xnT, T_mrg, T_mrgB = tk("xnT"), tk("mrg"), tk("mrgB")
        T_A = [tk(f"A{i}") for i in range(3)]
        T_xn = [tk("xn0"), tk("xn1")]
        T_ckvo = [tk("ckvo0"), tk("ckvo1")]; T_kro = [tk("kro0"), tk("kro1")]
        T_cqn2, T_ckvnb2, T_krb2 = [tk("cqn0"), tk("cqn1")], [tk("ckvnb0"), tk("ckvnb1")], [tk("krb0"), tk("krb1")]
        T_ropet = tk("ropet")
        T_cqnT, T_ckvnT, T_krTt = tk("cqnT"), tk("ckvnT"), tk("krTt")
        T_rq, T_rk = tk("rq"), tk("rk")
        T_PT = [tk(f"PT{i}") for i in range(4)]
        T_SC = [tk(f"SC{i}") for i in range(4)]
        T_cst = T_hb[3]
        stat_tk = {}

        def stat(name, n):
            if name not in stat_tk:
                off = stat_tk.get("_off", 0)
                stat_tk["_off"] = off + n
                assert off + n <= 64
                stat_tk[name] = (stats[:, off:off + n], tk("st_" + name))
            return stat_tk[name]

        def MM(out, lhsT, rhs, start, stop, rd, wr):
            P.op("pe", lambda e: e.matmul(out, lhsT=lhsT, rhs=rhs, start=start, stop=stop), reads=rd, writes=wr)

        def TR(out, in_, n, rd, wr):
            P.op("pe", lambda e: e.transpose(out=out, in_=in_, identity=ident[0:n, 0:n]), reads=rd + [T_ident], writes=wr)

        def ACT(out, in_, func, rd, wr, scale=1.0, accum=None, bias=None):
            kw = {}
            if accum is not None:
                kw["accum_out"] = accum
            if bias is not None:
                kw["bias"] = bias
            P.op("act", lambda e: e.activation(out=out, in_=in_, func=func, scale=scale, **kw), reads=rd, writes=wr)

        def CP(eng, out, in_, rd, wr):
            if eng == "act":
                P.op("act", lambda e: e.copy(out=out, in_=in_), reads=rd, writes=wr)
            else:
                P.op(eng, lambda e: e.tensor_copy(out=out, in_=in_), reads=rd, writes=wr)

        def TSC(eng, out, in0, s1, s2, op0, op1, rd, wr):
            if s2 is None:
                P.op(eng, lambda e: e.tensor_scalar(out=out, in0=in0, scalar1=s1, scalar2=None, op0=op0), reads=rd, writes=wr)
            else:
                P.op(eng, lambda e: e.tensor_scalar(out=out, in0=in0, scalar1=s1, scalar2=s2, op0=op0, op1=op1), reads=rd, writes=wr)

        def TT(eng, out, in0, in1, op, rd, wr):
            P.op(eng, lambda e: e.tensor_tensor(out=out, in0=in0, in1=in1, op=op), reads=rd, writes=wr)

        def STT(eng, out, in0, scalar, in1, op0, op1, rd, wr):
            P.op(eng, lambda e: e.scalar_tensor_tensor(out=out, in0=in0, scalar=scalar, in1=in1, op0=op0, op1=op1),
                 reads=rd, writes=wr)

        cp_rr = {"i": 0}

        def CPrr(out, in_, rd, wr):
            cp_rr["i"] ^= 1
            CP("act" if cp_rr["i"] else "dve", out, in_, rd, wr)

        def rstd(ss_ap, n, t_ss):
            TSC("dve", ss_ap, ss_ap, EPS, None, ALU.add, None, [t_ss], [t_ss])
            TT("pool", ss_ap, ss_ap, mhalf[0:ss_ap.shape[0], 0:n], ALU.pow, [t_ss, T_mh], [t_ss])

        P.dma("sp", [(ident_f[:], ident_d), (maskT[:], maskT_d)], writes=[T_ident, T_maskT], sig=T_ident)
        CP("dve", ident[:], ident_f[:], [T_ident], [T_ident])
        P.op("dve", lambda e: e.memset(ones[:], 1.0), writes=[T_ones])
        P.op("dve", lambda e: e.memset(mhalf[:], -0.5), writes=[T_mh])
        P.dma("sp", [(g_gm_bc[:], g_gm.partition_broadcast(128)), (g_fin_bc[:], g_fin.partition_broadcast(128)),
                     (g_kv_bc[:], g_kv.partition_broadcast(128))], writes=[T_gbc], sig=T_gbc)

        def rowview(g, n):
            return g.rearrange("(kc p) -> kc p", p=128)
        P.dma("sp", [(cst[0:8, 0:128], rowview(g_mix, 8)), (cst[8:16, 0:128], rowview(g_ffn, 8)),
                     (cst[16:24, 0:128], rowview(g_mem, 8)), (cst[24:27, 0:128], rowview(g_q, 3))], writes=[T_cst], sig=T_cst)
        pb, pt = ps_next()
        P.op("pe", lambda e, pb=pb: e.transpose(out=pb[:, 0:27], in_=cst[0:27, 0:128], identity=ident_f[0:27, 0:27]),
             reads=[T_cst, T_ident], writes=[pt])
        CP("dve", gcol[:, 0:27], pb[:, 0:27], [pt], [T_gcol])
        for g in range(8):
            P.dma("sp", [(cst[:, 0:128], gm_ws[g])], writes=[T_cst], sig=T_cst)
            pb, pt = ps_next()
            P.op("pe", lambda e, pb=pb: e.transpose(out=pb[:, 0:128], in_=cst[:, 0:128], identity=ident_f[:]),
                 reads=[T_cst, T_ident], writes=[pt])
            TT("dve", WsT[:, g, :], pb[:, 0:128], maskT[:], ALU.mult, [pt, T_maskT], [T_WsT])
        P.op("dve", lambda e: e.memset(cst[0:64, 0:512], 0.0), reads=[], writes=[T_cst])
        bp = []
        for g in range(8):
            for s in range(NS):
                bp.append((cst[16 * s:16 * s + 16, g * 64 + 16 * s:g * 64 + 16 * s + 16], gm_ws[g, 0:16, 0:16]))
        P.dma("sp", bp, writes=[T_cst], sig=T_cst)
        for g in range(8):
            pb, pt = ps_next()
            P.op("pe", lambda e, pb=pb, g=g: e.transpose(out=pb[0:64, 0:64], in_=cst[0:64, g * 64:(g + 1) * 64],
                                                         identity=ident_f[0:64, 0:64]),
                 reads=[T_cst, T_ident], writes=[pt])
            TT("dve", WsT_s[:, g, :], pb[0:64, 0:64], maskT[0:64, 0:64], ALU.mult, [pt, T_maskT], [T_WsT])
        P.dma("sp", [(cst[0:1, 0:1024], gm_bs.rearrange("g p -> (g p)").rearrange("(o n) -> o n", o=1))],
              writes=[T_cst], sig=T_cst)
        P.op("dve", lambda e: e.memset(b_row[:], 0.0), reads=[], writes=[T_brow])
        CP("dve", b_row[0:1].rearrange("o g p -> o (g p)"), cst[0:1, 0:1024], [T_cst], [T_brow])
        bp = []
        for s in range(NS):
            bp.append((cst[0:1, 0:512].rearrange("o (g p) -> o g p", g=8)[:, :, 16 * s:16 * s + 16],
                       gm_bs[:, 0:16].rearrange("(o g) p -> o g p", o=1)))
        P.dma("sp", bp, writes=[T_cst], sig=T_cst)
        P.op("dve", lambda e: e.memset(b_row_s[:], 0.0), reads=[], writes=[T_brow])
        CP("dve", b_row_s[0:1].rearrange("o g p -> o (g p)"), cst[0:1, 0:512], [T_cst], [T_brow])

        out_ops = []
        import os as _os
        STOP = int(_os.environ.get("KSTOP", "999"))
        stg = {"n": 0}

        class _Stop(Exception):
            pass

        def stage(name):
            P.label = name
            stg["n"] += 1
            if stg["n"] > STOP:
                print("STOP at stage", stg["n"], name)
                raise _Stop()

        def norm_transpose(nst, TS, src_aps, src_tks, col0, ss_name, dstT, T_dst, ncols_off=0):
            ss, t_ss = stat(ss_name, 4)
            banks = [ps_hold() for _ in range(4)]
            for s_ in range(nst):
                src, tsrc = src_aps[s_], src_tks[s_]
                xb, txb = xn_st[s_ % 2], T_xn[s_ % 2]
                ACT(xb[0:TS, :], src, AF.Square, [tsrc], [txb, t_ss], scale=1.0 / 32, accum=ss[0:TS, s_:s_ + 1])
                rstd(ss[0:TS, s_:s_ + 1], 1, t_ss)
                TSC("dve", xb[0:TS, :], src, ss[0:TS, s_:s_ + 1], None, ALU.mult, None, [tsrc, t_ss], [txb])
                for kc in range(8):
                    pb, pt = banks[kc // 2]
                    pbb = pb[:].bitcast(BF16)
                    o = (kc % 2) * 512 + s_ * TS
                    TR(pbb[:, o:o + TS], xb[0:TS, kc * 128:(kc + 1) * 128], TS, [txb], [pt])
            TTn = nst * TS
            ps_rel(*banks)
            stage("nt_transposes")
            for kc in range(8):
                pb, pt = banks[kc // 2]
                pbb = pb[:].bitcast(BF16)
                o = (kc % 2) * 512
                dst = dstT[:, kc, ncols_off:ncols_off + TTn]
                if kc % 2 == 0:
                    TSC("dve", dst, pbb[:, o:o + TTn], gcol[:, col0 + kc:col0 + kc + 1], None, ALU.mult, None,
                        [pt, T_gcol], [T_dst])
                else:
                    ACT(dst, pbb[:, o:o + TTn], AF.Copy, [pt, T_gcol], [T_dst], scale=gcol[:, col0 + kc:col0 + kc + 1])

        def fm_proj(wc, ocs, rhsT, T_rhs, KC, TTn, evac):
            for (ocl, oc) in ocs:
                pb, pt = ps_next()
                for kc in range(KC):
                    MM(pb[:, 0:TTn], wc.ap[:, kc, ocl * 128:(ocl + 1) * 128], rhsT[:, kc, 0:TTn], kc == 0, kc == KC - 1,
                       [wc.tk, T_rhs], [pt])
                evac(oc, pb, pt)

        def attention(h_list, qn_of, qr_of, nq, keytiles, out_of, scale, kind):
            nk_t = len(keytiles)
            items = [(hi, h, idx) for hi, h in enumerate(h_list) for idx in range(nk_t)]

            def issue_S(g):
                hi, h, idx = items[g]
                j, nk, c0, diag = keytiles[idx]
                b = g % 4
                sbk, stk = psb[b], ps_tk[b]
                qn, tqn = qn_of(h)
                qr, tqr, hb_ = qr_of(h)
                MM(sbk[0:nk, c0:nq], Kn[:, h, j * 128:j * 128 + nk], qn[:, c0:nq], True, False, [T_Kn, tqn], [stk])
                MM(sbk[0:nk, c0:nq], KrT[:, j * 128:j * 128 + nk], qr[:, c0:nq], False, True, [T_KrT, tqr], [stk])
                ACT(PT[b][0:nk, c0:nq], sbk[0:nk, c0:nq], AF.Exp, [stk], [T_PT[b]], scale=scale)
                if diag:
                    P.op("dve", lambda e, b=b, c0=c0: e.memset(PT[b][64:128, c0:c0 + 64], 0.0), reads=[], writes=[T_PT[b]])

            def issue_PV(g):
                hi, h, idx = items[g]
                j, nk, c0, diag = keytiles[idx]
                b = g % 4
                ob, ot = psb[4 + hi % 2], ps_tk[4 + hi % 2]
                db, dt_ = psb[6 + hi % 2], ps_tk[6 + hi % 2]
                MM(ob[:, c0:nq], V[0:nk, j, h * 128:(h + 1) * 128], PT[b][0:nk, c0:nq], idx == 0, idx == nk_t - 1,
                   [T_V, T_PT[b]], [ot])
                MM(db[:, c0:nq], ones[0:nk, :], PT[b][0:nk, c0:nq], idx == 0, idx == nk_t - 1, [T_ones, T_PT[b]], [dt_])
                if idx == nk_t - 1:
                    sc, tsc = SC[hi % 2], T_SC[hi % 2]
                    P.op("dve", lambda e, sc=sc, db=db: e.reciprocal(out=sc[:, 0:nq], in_=db[:, 0:nq]), reads=[dt_], writes=[tsc])
                    o_ap, o_tk = out_of(h)
                    TT("dve", o_ap, ob[:, 0:nq], sc[:, 0:nq], ALU.mult, [ot, tsc], [o_tk])
            LA = 3
            n_it = len(items)
            for g in range(min(LA, n_it)):
                issue_S(g)
            for g in range(n_it):
                issue_PV(g)
                if g + LA < n_it:
                    issue_S(g + LA)

        def mem_attention(qm, T_qm, q0, nq, out, T_out):
            items = [(hm, mt) for hm in range(4) for mt in range(2)]

            def bk(i):
                return psb[i], ps_tk[i]

            def issue_S(i):
                hm, mt = items[i]
                sbk, stk = bk(i % 2)
                for dc in range(2):
                    MM(sbk[:, 0:nq], KmT[:, hm * 2 + dc, mt * 128:(mt + 1) * 128], qm[:, hm * 2 + dc, q0:q0 + nq],
                       dc == 0, dc == 1, [T_KmT, T_qm], [stk])
                ACT(PT[i % 4][:, 0:nq], sbk[:, 0:nq], AF.Exp, [stk], [T_PT[i % 4]], scale=MEM_SCALE)

            def issue_PV(i):
                hm, mt = items[i]
                p = hm % 2
                b = i % 4
                for mc in range(2):
                    ob, ot = bk(2 + 3 * p + mc)
                    MM(ob[:, 0:nq], Vm[:, mt, hm * 256 + mc * 128:hm * 256 + (mc + 1) * 128], PT[b][:, 0:nq],
                       mt == 0, mt == 1, [T_Vm, T_PT[b]], [ot])
                db, dt_ = bk(4 + 3 * p)
                MM(db[:, 0:nq], ones[:, :], PT[b][:, 0:nq], mt == 0, mt == 1, [T_ones, T_PT[b]], [dt_])
                if mt == 1:
                    sc, tsc = SC[p], T_SC[p]
                    P.op("dve", lambda e, sc=sc, db=db: e.reciprocal(out=sc[:, 0:nq], in_=db[:, 0:nq]), reads=[dt_], writes=[tsc])
                    for mc in range(2):
                        ob, ot = bk(2 + 3 * p + mc)
                        TT("dve", out[:, hm * 2 + mc, q0:q0 + nq], ob[:, 0:nq], sc[:, 0:nq], ALU.mult, [ot, tsc], [T_out])
            LA = 2
            for i in range(LA):
                issue_S(i)
            for i in range(len(items)):
                issue_PV(i)
                if i + LA < len(items):
                    issue_S(i + LA)

        def build_kv(wkv, srcT, T_src, c0, ncols, key0):
            for h in range(H):
                pb, pt = ps_next()
                for kc in range(2):
                    MM(pb[:, 0:ncols], wkv.ap[:, kc, h * 128:(h + 1) * 128], srcT[:, kc, c0:c0 + ncols], kc == 0, kc == 1,
                       [wkv.tk, T_src], [pt])
                CPrr(Kn[:, h, key0:key0 + ncols], pb[:, 0:ncols], [pt], [T_Kn])
            nkt = (ncols + 127) // 128
            for t_ in range(nkt):
                nk = min(128, ncols - t_ * 128)
                j = key0 // 128 + t_
                for nh in range(2):
                    pb, pt = ps_next()
                    for kc in range(2):
                        MM(pb[0:nk, :], srcT[:, kc, c0 + t_ * 128:c0 + t_ * 128 + nk], wkv.ap[:, 2 + kc, nh * 512:(nh + 1) * 512],
                           kc == 0, kc == 1, [wkv.tk, T_src], [pt])
                    CPrr(V[0:nk, j, nh * 512:(nh + 1) * 512], pb[0:nk, :], [pt], [T_V])

        sg_h = [SC[2][:].bitcast(BF16)[:, 0:512], SC[2][:].bitcast(BF16)[:, 512:1024]]
        tmp_h = [SC[3][:].bitcast(BF16)[:, 0:512], SC[3][:].bitcast(BF16)[:, 512:1024]]
        T_sg = [tk("sg0"), tk("sg1")]
        T_tmp = [tk("tmp0"), tk("tmp1")]

        def branch_merge(gname, oT, T_o, first, TTn):
            for j in range(4):
                wc = w_get(f"{gname}{j}")
                for ocl in range(2):
                    oc = 2 * j + ocl
                    pg, tg = ps_next()
                    for kc in range(8):
                        MM(pg[:, 0:TTn], wc.ap[:, kc, ocl * 128:(ocl + 1) * 128], xnT[:, kc, 0:TTn], kc == 0, kc == 7,
                           [wc.tk, T_xnT], [tg])
                    sg, tsg = sg_h[oc % 2], T_sg[oc % 2]
                    ACT(sg[:, 0:TTn], pg[:, 0:TTn], AF.Sigmoid, [tg], [tsg])
                    pbk, tb = ps_next()
                    for kc in range(8):
                        MM(pbk[:, 0:TTn], wc.ap[:, kc, 256 + ocl * 128:256 + (ocl + 1) * 128], oT[:, kc, 0:TTn], kc == 0, kc == 7,
                           [wc.tk, T_o], [tb])
                    if first:
                        TT("dve", mrg[:, oc, 0:TTn], pbk[:, 0:TTn], sg[:, 0:TTn], ALU.mult, [tb, tsg], [T_mrg if oc < 4 else T_mrgB])
                    else:
                        tm_, ttm = tmp_h[oc % 2], T_tmp[oc % 2]
                        TT("dve", tm_[:, 0:TTn], pbk[:, 0:TTn], sg[:, 0:TTn], ALU.mult, [tb, tsg], [ttm])
                        tmg = T_mrg if oc < 4 else T_mrgB
                        TT("pool", mrg[:, oc, 0:TTn], mrg[:, oc, 0:TTn], tm_[:, 0:TTn], ALU.add, [ttm, tmg], [tmg])
                w_rel(wc)

        mrgx = mrg_f

        def token_tile(nst, TS, x_rows, y_rows, ckv_rows, kr_rows, rq_src, rk_src, is_sample, key0, tile_i, gv_rows=None,
                       pre=False, nxt=None):
            TTn = nst * TS
            WsT_u = WsT_s if is_sample else WsT
            brow_u = b_row_s if is_sample else b_row
            uT, T_u = AR[:, 0], T_A[0]
            vn, T_vn = AR[:, 1], T_A[1]
            vn_tm = AR[:, 1].rearrange("p a b -> p (a b)").rearrange("p (s d) -> p s d", s=4)
            oc_, T_oc = AR[:, 2], T_A[2]
            Qn, T_Qn = AR[:, 0], T_A[0]
            Qr, T_Qr = AR[:, 1], T_A[1]
            Qm, T_Qm = AR[:, 0], T_A[0]
            npre = min(2, nst) if pre else 0
            for s_ in range(npre, nst):
                P.dma("sp", [(HB[0:TS, s_, :], x_rows(s_))], writes=[T_hb[s_]], sig=T_hb[s_])
            P.dma("sp", [(rq[:, 0, 0:TTn], rq_src[0]), (rq[:, 1, 0:TTn], rq_src[1])], writes=[T_rq], sig=T_rq)
            P.dma("sp", [(rk[0:TS, 0, 0:nst, :], rk_src[0]), (rk[0:TS, 1, 0:nst, :], rk_src[1])], writes=[T_rk], sig=T_rk)
            norm_transpose(nst, TS, [mrgx[0:TS, s_, :] if s_ < npre else HB[0:TS, s_, :] for s_ in range(nst)],
                           [T_mrg if s_ < npre else T_hb[s_] for s_ in range(nst)], 0, "x", xnT, T_xnT)
            stage("A")
            wv0 = w_get("v0"); wv1 = w_get("v1")
            ssv, t_ssv = stat("v", 4)
            for s_ in range(nst):
                gv, T_gv = HB[0:TS, 0, :], T_hb[0]
                for nh, wv in enumerate((wv0, wv1)):
                    pb, pt = ps_next()
                    for kc in range(8):
                        MM(pb[0:TS, :], xnT[:, kc, s_ * TS:(s_ + 1) * TS], wv.ap[:, kc, :], kc == 0, kc == 7, [wv.tk, T_xnT], [pt])
                    ACT(gv[:, nh * 512:(nh + 1) * 512], pb[0:TS, :], AF.Gelu_apprx_tanh, [pt], [T_gv])
                ACT(xn_st[s_ % 2][0:TS, :], gv, AF.Square, [T_gv], [T_xn[s_ % 2], t_ssv], scale=1.0 / 32, accum=ssv[0:TS, s_:s_ + 1])
                rstd(ssv[0:TS, s_:s_ + 1], 1, t_ssv)
                if is_sample:
                    gvo, T_gvo = HB[0:TS, 2, :], T_hb[2]
                    STT("dve", gvo, gv, ssv[0:TS, s_:s_ + 1], g_gm_bc[0:TS, :], ALU.mult, ALU.mult, [T_gv, t_ssv, T_gbc], [T_gvo])
                    out_ops.append(P.dma("act", [(gv_rows, gvo)], reads=[T_gvo], sig=T_gvo))
                    CP("act", vn_tm[0:TS, s_, :], gvo, [T_gvo], [T_vn])
                else:
                    STT("dve", vn_tm[0:TS, s_, :], gv, ssv[0:TS, s_:s_ + 1], g_gm_bc[0:TS, :], ALU.mult, ALU.mult,
                        [T_gv, t_ssv, T_gbc], [T_vn])
            w_rel(wv0); w_rel(wv1)
            for j in range(2):
                wu = w_get(f"u{j}")

                def ev_u(oc, pb, pt):
                    ACT(uT[:, oc, 0:TTn], pb[:, 0:TTn], AF.Gelu_apprx_tanh, [pt], [T_u])
                fm_proj(wu, [(ocl, 4 * j + ocl) for ocl in range(4)], xnT, T_xnT, 8, TTn, ev_u)
                w_rel(wu)
            for g in range(8):
                pb, pt = ps_next()
                for s_ in range(nst):
                    MM(pb[:, s_ * TS:(s_ + 1) * TS], ones[:, 0:128], brow_u[:, g, :], True, False, [T_ones, T_brow], [pt])
                    MM(pb[:, s_ * TS:(s_ + 1) * TS], vn_tm[0:TS, s_, g * 128:(g + 1) * 128], WsT_u[0:TS, g, :], False, True,
                       [T_vn, T_WsT], [pt])
                TT("dve", oc_[:, g, 0:TTn], pb[:, 0:TTn], uT[:, g, 0:TTn], ALU.mult, [pt, T_u], [T_oc])
            branch_merge("ga", oc_, T_oc, True, TTn)
            stage("B")
            issue_prep(3)
            wc0 = w_get("c0"); wc1 = w_get("c1")
            ssc, t_ssc = stat("c", 8)
            bQ0, bQ1, bK0 = ps_hold(), ps_hold(), ps_hold()
            def c_stage1(s_):
                tm, T_tm = HB[0:TS, s_, 0:704], T_hb[s_]
                for jj, wc in enumerate((wc0, wc1)):
                    pb, pt = ps_next()
                    for kc in range(8):
                        MM(pb[0:TS, 0:352], xnT[:, kc, s_ * TS:(s_ + 1) * TS], wc.ap[:, kc, :], kc == 0, kc == 7, [wc.tk, T_xnT], [pt])
                    CPrr(tm[:, jj * 352:(jj + 1) * 352], pb[0:TS, 0:352], [pt], [T_tm])

            def c_stage2(s_):
                tm, T_tm = HB[0:TS, s_, 0:704], T_hb[s_]
                cq_n, T_cqn = cq_n2[s_ % 2], T_cqn2[s_ % 2]
                ckv_nb, T_ckvnb = ckv_nb2[s_ % 2], T_ckvnb2[s_ % 2]
                kr_b, T_krb = kr_b2[s_ % 2], T_krb2[s_ % 2]
                jk, T_jk = xn_st[s_ % 2], T_xn[s_ % 2]
                ssq = ssc[0:TS, 2 * s_:2 * s_ + 2]
                ACT(jk[0:TS, 0:QL], tm[:, 0:QL], AF.Square, [T_tm], [T_jk, t_ssc], scale=float(QL ** -0.5), accum=ssq[:, 0:1])
                ACT(jk[0:TS, 0:KVL], tm[:, QL:QL + KVL], AF.Square, [T_tm], [T_jk, t_ssc], scale=float(KVL ** -0.5), accum=ssq[:, 1:2])
                rstd(ssq, 2, t_ssc)
                TSC("dve", cq_n[0:TS, :], tm[:, 0:QL], ssq[:, 0:1], None, ALU.mult, None, [T_tm, t_ssc], [T_cqn])
                co, T_co = ckv_o[s_ % 2], T_ckvo[s_ % 2]
                STT("dve", co[0:TS, :], tm[:, QL:QL + KVL], ssq[:, 1:2], g_kv_bc[0:TS, :], ALU.mult, ALU.mult,
                    [T_tm, t_ssc, T_gbc], [T_co])
                out_ops.append(P.dma("act", [(ckv_rows(s_), co[0:TS, :])], reads=[T_co], sig=T_co))
                CP("act", ckv_nb[0:TS, :], co[0:TS, :], [T_co], [T_ckvnb])
                xk = tm[:, 640:704]
                ko, T_ko = kr_o[s_ % 2], T_kro[s_ % 2]
                TT("dve", rope_t[0:TS, 0, :], xk, rk[0:TS, 0, s_, :], ALU.mult, [T_tm, T_rk], [T_ropet])
                TT("dve", rope_t[0:TS, 1, :], xk, rk[0:TS, 1, s_, :], ALU.mult, [T_tm, T_rk], [T_ropet])
                TT("dve", ko[0:TS, 0:32], rope_t[0:TS, 0, 0:32], rope_t[0:TS, 1, 32:64], ALU.subtract, [T_ropet], [T_ko])
                TT("dve", ko[0:TS, 32:64], rope_t[0:TS, 1, 0:32], rope_t[0:TS, 0, 32:64], ALU.add, [T_ropet], [T_ko])
                out_ops.append(P.dma("act", [(kr_rows(s_), ko[0:TS, :])], reads=[T_ko], sig=T_ko))
                CP("act", kr_b[0:TS, 0:64], ko[0:TS, :], [T_ko], [T_krb])
                CP("act", kr_b[0:TS, 64:128], ko[0:TS, :], [T_ko], [T_krb])

            def c_stage3(s_):
                cq_n, T_cqn = cq_n2[s_ % 2], T_cqn2[s_ % 2]
                ckv_nb, T_ckvnb = ckv_nb2[s_ % 2], T_ckvnb2[s_ % 2]
                kr_b, T_krb = kr_b2[s_ % 2], T_krb2[s_ % 2]
                for kc in range(3):
                    pb, pt = (bQ0, bQ1)[kc // 2]
                    o = (kc % 2) * 512 + s_ * TS
                    TR(pb[:].bitcast(BF16)[:, o:o + TS], cq_n[0:TS, kc * 128:(kc + 1) * 128], TS, [T_cqn], [pt])
                for kc in range(2):
                    o = kc * 512 + s_ * TS
                    TR(bK0[0][:].bitcast(BF16)[:, o:o + TS], ckv_nb[0:TS, kc * 128:(kc + 1) * 128], TS, [T_ckvnb], [bK0[1]])
                o = 512 + s_ * TS
                TR(bQ1[0][:].bitcast(BF16)[:, o:o + TS], kr_b[0:TS, :], TS, [T_krb], [bQ1[1]])

            if nst == 4:
                c_stage1(0); c_stage1(1); c_stage2(0); c_stage1(2); c_stage3(0); c_stage2(1)
                c_stage1(3); c_stage3(1); c_stage2(2); c_stage3(2); c_stage2(3); c_stage3(3)
            else:
                for s_ in range(nst):
                    c_stage1(s_)
                    c_stage2(s_)
                    c_stage3(s_)
            w_rel(wc0); w_rel(wc1)
            ps_rel(bQ0, bQ1, bK0)
            for kc in range(3):
                pb, pt = (bQ0, bQ1)[kc // 2]
                o = (kc % 2) * 512
                TSC("dve", cq_nT[:, kc, 0:TTn], pb[:].bitcast(BF16)[:, o:o + TTn], gcol[:, 24 + kc:25 + kc], None, ALU.mult, None,
                    [pt, T_gcol], [T_cqnT])
            for kc in range(2):
                CP("act", ckv_nT[:, kc, 0:TTn], bK0[0][:].bitcast(BF16)[:, kc * 512:kc * 512 + TTn], [bK0[1]], [T_ckvnT])
            if is_sample:
                CP("act", krT_t[:, 0:TTn], bQ1[0][:].bitcast(BF16)[:, 512:512 + TTn], [bQ1[1]], [T_krTt])
            else:
                CP("act", KrT[:, key0:key0 + TTn], bQ1[0][:].bitcast(BF16)[:, 512:512 + TTn], [bQ1[1]], [T_KrT])
            wq0 = w_get("q0")

            def ev_qn(h, pb, pt):
                CPrr(Qn[:, h, 0:TTn], pb[:, 0:TTn], [pt], [T_Qn])
            fm_proj(wq0, [(h, h) for h in range(H)], cq_nT, T_cqnT, 3, TTn, ev_qn)
            w_rel(wq0)
            wq1 = w_get("q1")
            for hp in range(4):
                px, tx = ps_next()
                pxs, txs = ps_next()
                for kc in range(3):
                    MM(px[:, 0:TTn], wq1.ap[:, kc, hp * 128:(hp + 1) * 128], cq_nT[:, kc, 0:TTn], kc == 0, kc == 2, [wq1.tk, T_cqnT], [tx])
                for kc in range(3):
                    MM(pxs[:, 0:TTn], wq1.ap[:, kc, 512 + hp * 128:512 + (hp + 1) * 128], cq_nT[:, kc, 0:TTn], kc == 0, kc == 2,
                       [wq1.tk, T_cqnT], [txs])
                TT("dve", SC[0][:, 0:TTn], px[:, 0:TTn], rq[:, 0, 0:TTn], ALU.mult, [tx, T_rq], [T_SC[0]])
                TT("dve", SC[1][:, 0:TTn], pxs[:, 0:TTn], rq[:, 1, 0:TTn], ALU.mult, [txs, T_rq], [T_SC[1]])
                TT("pool", Qr[0:64, 2 * hp, 0:TTn], SC[0][0:64, 0:TTn], SC[1][0:64, 0:TTn], ALU.add, [T_SC[0], T_SC[1]], [T_Qr])
                TT("pool", Qr[64:128, 2 * hp + 1, 0:TTn], SC[0][64:128, 0:TTn], SC[1][64:128, 0:TTn], ALU.add, [T_SC[0], T_SC[1]], [T_Qr])
                P.op("pool", lambda e, hp=hp: e.memset(Qr[64:128, 2 * hp, 0:TTn], 0.0), reads=[], writes=[T_Qr])
                P.op("pool", lambda e, hp=hp: e.memset(Qr[0:64, 2 * hp + 1, 0:TTn], 0.0), reads=[], writes=[T_Qr])
            w_rel(wq1)
            stage("C")
            issue_prep(4)
            wkv = w_get("kv")
            if not is_sample:
                build_kv(wkv, ckv_nT, T_ckvnT, 0, TTn, key0)
                w_rel(wkv)
                kts = []
                for j in range(4 * tile_i):
                    kts.append((j, 128, 0, False))
                for r in range(4):
                    kts.append((4 * tile_i + r, 128, 128 * r, True))
                attention(list(range(H)), lambda h: (Qn[:, h, :], T_Qn), lambda h: (Qr[:, h, :], T_Qr, 0),
                          TTn, kts, lambda h: (oc_[:, h, 0:TTn], T_oc), MLA_SCALE, "mla")
            else:
                for s in range(NS):
                    hb23 = HB[:, 2:4, :].rearrange("p a (t c) -> p (a t) c", c=KVL)
                    P.dma("sp", [(hb23[:, 0:NKT_S, :], c_ckv[s].rearrange("(t p) c -> p t c", p=128))],
                          writes=[T_hb[2], T_hb[3]], sig=T_hb[2])
                    cpb = HB[:, 0, :].bitcast(BF16).rearrange("p (t c) -> p t c", c=KVL)
                    cT = HB[:, 1, :].bitcast(BF16).rearrange("p (k n) -> p k n", k=2)
                    CP("dve", cpb[:, 0:NKT_S, :], hb23[:, 0:NKT_S, :], [T_hb[2], T_hb[3]], [T_hb[0]])
                    for t_ in range(NKT_S):
                        for kc in range(2):
                            if (t_ * 2 + kc) % 8 == 0:
                                pb, pt = ps_next()
                            o = ((t_ * 2 + kc) % 8) * 128
                            TR(pb[:].bitcast(BF16)[:, o:o + 128], cpb[:, t_, kc * 128:(kc + 1) * 128], 128, [T_hb[0]], [pt])
                            if (t_ * 2 + kc) % 8 == 7:
                                for tt in range(4):
                                    for k2 in range(2):
                                        oo = (tt * 2 + k2) * 128
                                        tg = t_ - 3 + tt
                                        CPrr(cT[:, k2, tg * 128:(tg + 1) * 128], pb[:].bitcast(BF16)[:, oo:oo + 128], [pt], [T_hb[1]])
                    for cb in range(PAST // 512):
                        build_kv(wkv, cT, T_hb[1], cb * 512, 512, cb * 512)
                    build_kv(wkv, ckv_nT, T_ckvnT, DEC * s, DEC, PAST)
                    krl = HB[:, 2, :].rearrange("p (t c) -> p t c", c=ROPE)
                    P.dma("sp", [(krl[:, 0:NKT_S, :], c_kr[s].rearrange("(t p) c -> p t c", p=128))], writes=[T_hb[2]], sig=T_hb[2])
                    kpb = xn_st[0][:].rearrange("p (t c) -> p t c", c=128)
                    CP("dve", kpb[:, 0:NKT_S, 0:64], krl[:, 0:NKT_S, :], [T_hb[2]], [T_xn[0]])
                    CP("act", kpb[:, 0:NKT_S, 64:128], krl[:, 0:NKT_S, :], [T_hb[2]], [T_xn[0]])
                    for t_ in range(NKT_S):
                        if t_ % 8 == 0:
                            pb, pt = ps_next()
                        o = (t_ % 8) * 128
                        TR(pb[:].bitcast(BF16)[:, o:o + 128], kpb[:, t_, :], 128, [T_xn[0]], [pt])
                        if t_ % 8 == 7 or t_ == NKT_S - 1:
                            n_ = (t_ % 8) + 1
                            t0_ = t_ - (t_ % 8)
                            CPrr(KrT[:, t0_ * 128:(t0_ + n_) * 128], pb[:].bitcast(BF16)[:, 0:n_ * 128], [pt], [T_KrT])
                    CP("act", KrT[:, PAST:PAST + DEC], krT_t[:, DEC * s:DEC * (s + 1)], [T_krTt], [T_KrT])
                    kts = [(j, 128, 0, False) for j in range(NKT_S)] + [(NKT_S, DEC, 0, False)]
                    attention(list(range(H)),
                              lambda h, s=s: (Qn[:, h, DEC * s:DEC * (s + 1)], T_Qn),
                              lambda h, s=s: (Qr[:, h, DEC * s:DEC * (s + 1)], T_Qr, 0),
                              DEC, kts, lambda h, s=s: (oc_[:, h, DEC * s:DEC * (s + 1)], T_oc), MLA_SCALE, "mla")
                w_rel(wkv)
            stage("D")
            issue_prep(5)
            branch_merge("gb", oc_, T_oc, False, TTn)
            for j in range(2):
                wq = w_get(f"qm{j}")

                def ev_qm(oc, pb, pt):
                    CPrr(Qm[:, oc, 0:TTn], pb[:, 0:TTn], [pt], [T_Qm])
                fm_proj(wq, [(ocl, 4 * j + ocl) for ocl in range(4)], xnT, T_xnT, 8, TTn, ev_qm)
                w_rel(wq)
            if not is_sample:
                mem_attention(Qm, T_Qm, 0, TTn, oc_, T_oc)
            else:
                for s in range(NS):
                    load_mem_cache(s)
                    mem_attention(Qm, T_Qm, DEC * s, DEC, oc_, T_oc)
            branch_merge("gc", oc_, T_oc, False, TTn)
            stage("E")
            for s_ in range(nst):
                P.dma("sp", [(HB[0:TS, s_, :], x_rows(s_))], writes=[T_hb[s_]], sig=T_hb[s_])
            wo2 = [w_get("o0"), w_get("o1")]
            ssh, t_ssh = stat("h", 4)
            hbanks = [ps_hold() for _ in range(4)]

            def g_mm(s_):
                for nh in range(2):
                    wo = wo2[nh]
                    pb, pt = ps_next()
                    for kc in range(8):
                        MM(pb[0:TS, :], mrg[:, kc, s_ * TS:(s_ + 1) * TS], wo.ap[:, kc, :], kc == 0, kc == 7,
                           [wo.tk, T_mrg if kc < 4 else T_mrgB], [pt])
                    hsl = HB[0:TS, s_, nh * 512:(nh + 1) * 512]
                    TT("dve", hsl, pb[0:TS, :], hsl, ALU.add, [pt, T_hb[s_]], [T_hb[s_]])

            def g_norm(s_):
                src, tsrc = HB[0:TS, s_, :], T_hb[s_]
                xb, txb = xn_st[s_ % 2], T_xn[s_ % 2]
                ACT(xb[0:TS, :], src, AF.Square, [tsrc], [txb, t_ssh], scale=1.0 / 32, accum=ssh[0:TS, s_:s_ + 1])
                rstd(ssh[0:TS, s_:s_ + 1], 1, t_ssh)
                TSC("dve", xb[0:TS, :], src, ssh[0:TS, s_:s_ + 1], None, ALU.mult, None, [tsrc, t_ssh], [txb])

            def g_tr(s_):
                xb, txb = xn_st[s_ % 2], T_xn[s_ % 2]
                for kc in range(8):
                    pb, pt = hbanks[kc // 2]
                    o = (kc % 2) * 512 + s_ * TS
                    TR(pb[:].bitcast(BF16)[:, o:o + TS], xb[0:TS, kc * 128:(kc + 1) * 128], TS, [txb], [pt])

            for s_ in range(nst):
                g_mm(s_)
                g_norm(s_)
                if s_ >= 1:
                    g_tr(s_ - 1)
            g_tr(nst - 1)
            w_rel(wo2[0]); w_rel(wo2[1])
            ps_rel(*hbanks)
            stage("G")
            for kc in range(8):
                pb, pt = hbanks[kc // 2]
                pbb = pb[:].bitcast(BF16)
                o = (kc % 2) * 512
                dst = xnT[:, kc, 0:TTn]
                if kc % 2 == 0:
                    TSC("dve", dst, pbb[:, o:o + TTn], gcol[:, 8 + kc:8 + kc + 1], None, ALU.mult, None, [pt, T_gcol], [T_xnT])
                else:
                    ACT(dst, pbb[:, o:o + TTn], AF.Copy, [pt, T_gcol], [T_xnT], scale=gcol[:, 8 + kc:8 + kc + 1])
            aT = AR[:].rearrange("p a b c -> p (a b) c")
            for j in range(11):
                wf = w_get(f"f{j}")
                for ocl in range(2):
                    oc = 2 * j + ocl
                    pg, tg = ps_next()
                    pu, tu = ps_next()
                    for kc in range(8):
                        MM(pg[:, 0:TTn], wf.ap[:, kc, ocl * 128:(ocl + 1) * 128], xnT[:, kc, 0:TTn], kc == 0, kc == 7, [wf.tk, T_xnT], [tg])
                    for kc in range(8):
                        MM(pu[:, 0:TTn], wf.ap[:, kc, 256 + ocl * 128:256 + (ocl + 1) * 128], xnT[:, kc, 0:TTn], kc == 0, kc == 7,
                           [wf.tk, T_xnT], [tu])
                    sgb, tsg = SC[oc % 2], T_SC[oc % 2]
                    sg = sgb[:].bitcast(BF16)
                    ACT(sg[:, 0:TTn], pg[:, 0:TTn], AF.Silu, [tg], [tsg])
                    TT("dve", aT[:, oc, 0:TTn], pu[:, 0:TTn], sg[:, 0:TTn], ALU.mult, [tu, tsg], [T_A[oc // 8]])
                w_rel(wf)
            if nxt is not None:
                n_nst, n_TS, n_rows = nxt
                P.dma("sp", [(mrgx[0:n_TS, s_, :], n_rows(s_)) for s_ in range(min(2, n_nst))], writes=[T_mrg], sig=T_mrg)
            for nh in range(2):
                banks = [ps_hold() for _ in range(nst)]
                for kg in range(3):
                    wd = w_get(f"fd{nh}{kg}")
                    for s_ in range(nst):
                        pb, pt = banks[s_]
                        for kcl in range(KG[kg]):
                            kc = kg * 8 + kcl
                            MM(pb[0:TS, :], aT[:, kc, s_ * TS:(s_ + 1) * TS], wd.ap[:, kcl, :], kc == 0, kc == 21,
                               [wd.tk, T_A[kc // 8]], [pt])
                    w_rel(wd)
                ps_rel(*banks)
                for s_ in range(nst):
                    pb, pt = banks[s_]
                    hsl = HB[0:TS, s_, nh * 512:(nh + 1) * 512]
                    TT("dve", hsl, pb[0:TS, :], hsl, ALU.add, [pt, T_hb[s_]], [T_hb[s_]])
            stage("H")
            ssf, t_ssf = stat("f", 4)
            T_ys = [T_mrg, T_mrgB]
            pend = []
            for s_ in range(nst):
                hs = HB[0:TS, s_, :]
                if s_ < 2:
                    ys, tys = mrg_f[0:TS, s_, :], T_ys[s_]
                else:
                    ys, tys = hs, T_hb[s_]
                ACT(xn_st[s_ % 2][0:TS, :], hs, AF.Square, [T_hb[s_]], [T_xn[s_ % 2], t_ssf], scale=1.0 / 32, accum=ssf[0:TS, s_:s_ + 1])
                rstd(ssf[0:TS, s_:s_ + 1], 1, t_ssf)
                STT("dve", ys, hs, ssf[0:TS, s_:s_ + 1], g_fin_bc[0:TS, :], ALU.mult, ALU.mult, [T_hb[s_], t_ssf, T_gbc], [tys])
                pend.append((s_, ys, tys))
            for (s_, ys, tys) in pend:
                out_ops.append(P.dma("act", [(y_rows(s_), ys)], reads=[tys], sig=tys))

        def kmT_from_tokmajor(kb, T_kb, mt):
            for half in range(2):
                pb, pt = ps_next()
                for q in range(4):
                    hd = half * 4 + q
                    TR(pb[:].bitcast(BF16)[:, q * 128:(q + 1) * 128], kb[:, hd * 128:(hd + 1) * 128], 128, [T_kb], [pt])
                for q in range(4):
                    hd = half * 4 + q
                    CPrr(KmT[:, hd, mt * 128:(mt + 1) * 128], pb[:].bitcast(BF16)[:, q * 128:(q + 1) * 128], [pt], [T_KmT])

        def load_mem_cache(s):
            P.dma("sp", [(HB[:, 2, :], c_mk[s, 0:128, :]), (HB[:, 3, :], c_mk[s, 128:256, :])],
                  writes=[T_hb[2], T_hb[3]], sig=T_hb[2])
            for mt in range(2):
                CP("dve" if mt else "act", xn_st[mt][:], HB[:, 2 + mt, :], [T_hb[2], T_hb[3]], [T_xn[mt]])
                kmT_from_tokmajor(xn_st[mt], T_xn[mt], mt)
            P.dma("sp", [(HB[:, 2, :], c_mv[s, 0:128, :]), (HB[:, 3, :], c_mv[s, 128:256, :])],
                  writes=[T_hb[2], T_hb[3]], sig=T_hb[2])
            for mt in range(2):
                CP("dve" if mt else "act", Vm[:, mt, :], HB[:, 2 + mt, :], [T_hb[2], T_hb[3]], [T_Vm])

        def prompt_mem_kv(b):
            P.dma("sp", [(HB[:, 0, :], mem_p[b, 0:128, :]), (HB[:, 1, :], mem_p[b, 128:256, :])],
                  writes=[T_hb[0], T_hb[1]], sig=T_hb[0])
            stage("mem_load")
            norm_transpose(2, 128, [HB[:, 0, :], HB[:, 1, :]], [T_hb[0], T_hb[1]], 16, "m", xnT, T_xnT)
            stage("mem_norm")
            for c in range(4):
                wm = w_get(f"m{c}")
                kv, half = c // 2, c % 2
                for mt in range(2):
                    pb, pt = ps_next()
                    for kc in range(8):
                        MM(pb[:, :], xnT[:, kc, mt * 128:(mt + 1) * 128], wm.ap[:, kc, :], kc == 0, kc == 7, [wm.tk, T_xnT], [pt])
                    CPrr(HB[:, 2 + mt, half * 512:(half + 1) * 512], pb[:, :], [pt], [T_hb[2 + mt]])
                w_rel(wm)
                if half == 1:
                    dst = mk_p if kv == 0 else mv_p
                    for mt in range(2):
                        out_ops.append(P.dma("act", [(dst[b * NMEM + mt * 128:b * NMEM + (mt + 1) * 128, :], HB[:, 2 + mt, :])],
                                             reads=[T_hb[2 + mt]], sig=T_hb[2 + mt]))
                        if kv == 0:
                            CP("act", xn_st[mt][:], HB[:, 2 + mt, :], [T_hb[2 + mt]], [T_xn[mt]])
                            kmT_from_tokmajor(xn_st[mt], T_xn[mt], mt)
                        else:
                            CP("act", Vm[:, mt, :], HB[:, 2 + mt, :], [T_hb[2 + mt]], [T_Vm])

        wstate["released"] = 0
        try:
          stage("consts")
          w_load_next_if_room()
          stage("wload")
          tiles = []
          for b in range(NP):
            for ti in range(NT):
                r0 = b * SEQ + ti * 512
                tiles.append(dict(b=b, ti=ti, r0=r0, nst=4, TS=128,
                                  x_rows=(lambda s_, r0=r0: x_p[r0 + s_ * 128:r0 + (s_ + 1) * 128, :])))
          tiles.append(dict(b=-1, ti=0, r0=0, nst=1, TS=TS_S, x_rows=(lambda s_: x_s[:, :])))
          for k, td in enumerate(tiles):
            nx = tiles[k + 1] if k + 1 < len(tiles) else None
            nxt = (nx["nst"], nx["TS"], nx["x_rows"]) if nx is not None else None
            pre = (k > 0) and bool(_os.environ.get("KPRE"))
            nxt = nxt if _os.environ.get("KPRE") else None
            if td["b"] >= 0:
                b, ti, r0 = td["b"], td["ti"], td["r0"]
                if ti == 0:
                    prompt_mem_kv(b)
                    stage("memkv")
                    issue_prep(2)
                token_tile(
                    4, 128, td["x_rows"],
                    lambda s_, r0=r0: y_p[r0 + s_ * 128:r0 + (s_ + 1) * 128, :],
                    lambda s_, r0=r0: ckv_p[r0 + s_ * 128:r0 + (s_ + 1) * 128, :],
                    lambda s_, r0=r0: kr_p[r0 + s_ * 128:r0 + (s_ + 1) * 128, :],
                    (rq_p[0, :, ti * 512:(ti + 1) * 512], rq_p[1, :, ti * 512:(ti + 1) * 512]),
                    (rk_p[0, ti * 512:(ti + 1) * 512, :].rearrange("(s p) c -> p s c", p=128),
                     rk_p[1, ti * 512:(ti + 1) * 512, :].rearrange("(s p) c -> p s c", p=128)),
                    False, ti * 512, ti, pre=pre, nxt=nxt)
            else:
                token_tile(
                    1, TS_S, td["x_rows"], lambda s_: y_s[:, :], lambda s_: ckv_s[:, :], lambda s_: kr_s[:, :],
                    (rq_s[0], rq_s[1]),
                    (rk_s[0].rearrange("(s p) c -> p s c", s=1), rk_s[1].rearrange("(s p) c -> p s c", s=1)),
                    True, 0, 0, gv_rows=gv_s[:, :], pre=pre, nxt=None)
          assert wstate["next_get"] == len(wseq)
        except _Stop:
            pass
        if _os.environ.get("KDUMP"):
            with open(_os.environ["KDUMP"], "w") as fh:
                for e in ENGS:
                    for i, o in enumerate(P.ops[e]):
                        fh.write(f"{e} {i} {o.label} {int(o.signal)} {len(o.deps)}\n")
        P.finalize(out_ops)
    return nc


def _rope_tables(pos):
    inv = (np.float32(10000.0) ** (-np.arange(0, ROPE, 2, dtype=np.float32) / np.float32(ROPE))).astype(np.float32)
    ang = pos.astype(np.float32)[:, None] * inv[None, :]
    cos, sin = np.cos(ang).astype(np.float32), np.sin(ang).astype(np.float32)
    cos64 = np.concatenate([cos, cos], axis=1)
    sin64 = np.concatenate([sin, sin], axis=1)
    sgn = np.concatenate([-np.ones(32, np.float32), np.ones(32, np.float32)])
    rq = np.stack([np.tile(cos64.T, (2, 1)), np.tile((sin64 * sgn[None, :]).T, (2, 1))]).astype(np.float32)
    rk = np.stack([cos64, sin64]).astype(np.float32)
    return np.ascontiguousarray(rq), np.ascontiguousarray(rk)


def make_in_maps(inp, n_cores, NP, SEQ, NS, PAST):
    f = lambda a: np.ascontiguousarray(np.asarray(a, dtype=np.float32))
    rq_p, rk_p = _rope_tables(np.arange(SEQ))
    rq_s, rk_s = _rope_tables(PAST + (np.arange(NS * DEC) % DEC))
    qi, pi = np.meshgrid(np.arange(128), np.arange(128), indexing="ij")
    shared = {
        "g_mix": f(inp["norm_mix_g"][0]), "g_gm": f(inp["gm_norm_g"][0]), "g_q": f(inp["mla_q_norm_g"][0]),
        "g_kv": f(inp["mla_kv_norm_g"][0]), "g_mem": f(inp["mem_norm_g"][0]), "g_ffn": f(inp["norm_ffn_g"][0]),
        "g_fin": f(inp["final_norm_g"]), "w_in": f(inp["w_in"][0]), "gm_ws": f(inp["gm_ws"][0]), "gm_bs": f(inp["gm_bs"][0]),
        "w_uq": f(np.asarray(inp["mla_w_uq"][0]).reshape(QL, H * 192)), "w_uk": f(np.asarray(inp["mla_w_uk"][0]).reshape(KVL, D)),
        "w_uv": f(np.asarray(inp["mla_w_uv"][0]).reshape(KVL, D)), "w_mkv": f(inp["mem_w_kv"][0]),
        "w_bgm": f(inp["w_br_gm"][0]), "w_bmla": f(inp["w_br_mla"][0]), "w_bmem": f(inp["w_br_mem"][0]), "w_o": f(inp["w_out"][0]),
        "w_g": f(inp["ffn_w_gate"][0]), "w_u": f(inp["ffn_w_up"][0]), "w_d": f(inp["ffn_w_down"][0]),
        "ident": np.eye(128, dtype=np.float32), "maskT": (qi <= pi).astype(np.float32),
        "rq_p": rq_p, "rk_p": rk_p, "rq_s": rq_s, "rk_s": rk_s,
    }
    maps = []
    for c in range(n_cores):
        m = dict(shared)
        m["x_p"] = f(np.asarray(inp["x_prompt"])[c * NP:(c + 1) * NP].reshape(NP * SEQ, D))
        m["x_s"] = f(np.asarray(inp["x_sample"])[c * NS:(c + 1) * NS].reshape(NS * DEC, D))
        m["c_ckv"] = f(np.asarray(inp["cache_mla_ckv"])[0, c * NS:(c + 1) * NS])
        m["c_kr"] = f(np.asarray(inp["cache_mla_krope"])[0, c * NS:(c + 1) * NS])
        m["c_mk"] = f(np.asarray(inp["cache_mem_k"])[0, c * NS:(c + 1) * NS].reshape(NS, NMEM, D))
        m["c_mv"] = f(np.asarray(inp["cache_mem_v"])[0, c * NS:(c + 1) * NS].reshape(NS, NMEM, D))
        m["mem_p"] = f(np.asarray(inp["mem_prompt"])[c * NP:(c + 1) * NP])
        maps.append(m)
    return maps


def gather_outputs(results, n_cores, NP, SEQ, NS):
    cat = lambda k: np.concatenate([np.asarray(r[k], dtype=np.float32) for r in results], axis=0)
    B, BS = n_cores * NP, n_cores * NS
    return (cat("y_p").reshape(B, SEQ, D), cat("y_s").reshape(BS, DEC, D),
            cat("ckv_p").reshape(1, B, SEQ, KVL), cat("kr_p").reshape(1, B, SEQ, ROPE),
            cat("mk_p").reshape(1, B, NMEM, 4, 256), cat("mv_p").reshape(1, B, NMEM, 4, 256),
            cat("ckv_s").reshape(1, BS, DEC, KVL), cat("kr_s").reshape(1, BS, DEC, ROPE),
            cat("gv_s").reshape(1, BS, DEC, D))


def kernel(**inputs):
    B, SEQ, _ = inputs["x_prompt"].shape
    BS = inputs["x_sample"].shape[0]
    PAST = inputs["cache_mla_ckv"].shape[2]
    NP, NS = B // N_CORES, BS // N_CORES
    nc = build_program(NP, SEQ, NS, PAST)
    maps = make_in_maps(inputs, N_CORES, NP, SEQ, NS, PAST)
    res = run_bass_kernel_spmd(nc, maps, core_ids=list(range(N_CORES)))
    return gather_outputs(res.results, N_CORES, NP, SEQ, NS)
```
